# Optimizing a Trainium2 kernel written in Bass

```python
import jax, jax.numpy as jnp
from jax import lax
import numpy as np

D_MODEL = 1024
BATCH = 4
SEQ = 8192
DEPTH = 1

CONV_WIDTH = 1024
CONV_SIZE = 31
HEAD_DIM = 64
HEADS_PER_GROUP = 8
DILATION_GROUPS = ((128, 1), (512, 4), (2048, 16))
N_GROUPS = len(DILATION_GROUPS)
N_ATT_HEADS = N_GROUPS * HEADS_PER_GROUP
ATT_QKV = N_ATT_HEADS * HEAD_DIM
ATT_OUT = HEADS_PER_GROUP * HEAD_DIM
ROT_DIM = HEAD_DIM // 4
ROPE_THETA = 500000.0
BLOCK = 128
MAX_POS_OFFSET = 4096
EPS = 1e-6
NEG_INF = -1e30

IN_SPLITS = (CONV_WIDTH, CONV_WIDTH, CONV_WIDTH,
             ATT_QKV, ATT_QKV, ATT_QKV, ATT_OUT,
             D_MODEL, D_MODEL)
IN_COLS = sum(IN_SPLITS)

kernel_name = "hybrid_conformer_conv_dilated_attention_gated_merge"


def _rmsnorm(x, g):
    xf = x.astype(jnp.float32)
    y = xf * lax.rsqrt(jnp.mean(xf * xf, axis=-1, keepdims=True) + EPS)
    return (y * g.astype(jnp.float32)).astype(x.dtype)


def _layernorm(x, g, b):
    xf = x.astype(jnp.float32)
    mu = jnp.mean(xf, axis=-1, keepdims=True)
    var = jnp.mean(jnp.square(xf - mu), axis=-1, keepdims=True)
    y = (xf - mu) * lax.rsqrt(var + EPS)
    return (y * g.astype(jnp.float32) + b.astype(jnp.float32)).astype(x.dtype)


def _partial_rope(t, positions):
    half = ROT_DIM // 2
    inv_freq = ROPE_THETA ** (-(jnp.arange(half, dtype=jnp.float32) * 2.0 / ROT_DIM))
    ang = positions.astype(jnp.float32)[..., None] * inv_freq
    cos = jnp.cos(ang)[:, :, None, :]
    sin = jnp.sin(ang)[:, :, None, :]
    tf = t.astype(jnp.float32)
    t1, t2 = tf[..., :half], tf[..., half:ROT_DIM]
    out = jnp.concatenate([t1 * cos - t2 * sin, t2 * cos + t1 * sin, tf[..., ROT_DIM:]], axis=-1)
    return out.astype(t.dtype)


def _dilated_window_group(q, k, v, window, dilation):
    b, s, h, e = q.shape
    L = s // dilation
    w_sub = window // dilation
    nb = -(-L // BLOCK)
    lp = nb * BLOCK

    def to_sub(t):
        return t.reshape(b, L, dilation, h, e).transpose(0, 2, 3, 1, 4)

    qs, ks, vs = to_sub(q), to_sub(k), to_sub(v)
    qs = jnp.pad(qs, ((0, 0), (0, 0), (0, 0), (0, lp - L), (0, 0)))
    ks = jnp.pad(ks, ((0, 0), (0, 0), (0, 0), (BLOCK, lp - L), (0, 0)))
    vs = jnp.pad(vs, ((0, 0), (0, 0), (0, 0), (BLOCK, lp - L), (0, 0)))
    qb = qs.reshape(b, dilation, h, nb, BLOCK, e)

    def band(t):
        prev = t[:, :, :, :lp].reshape(b, dilation, h, nb, BLOCK, e)
        cur = t[:, :, :, BLOCK:].reshape(b, dilation, h, nb, BLOCK, e)
        return jnp.concatenate([prev, cur], axis=-2)

    kb, vb = band(ks), band(vs)
    scores = jnp.einsum('bdhnqe,bdhnke->bdhnqk', qb.astype(jnp.float32),
                        kb.astype(jnp.float32)) * (e ** -0.5)
    qi = jnp.arange(BLOCK)[:, None]
    kj = jnp.arange(2 * BLOCK)[None, :]
    dist = qi + BLOCK - kj
    key_idx = jnp.arange(nb)[:, None, None] * BLOCK - BLOCK + kj[None]
    mask = (dist >= 0) & (dist <= w_sub) & (key_idx >= 0)
    scores = jnp.where(mask, scores, NEG_INF)
    m = jnp.max(scores, axis=-1)
    p = jnp.exp(scores - m[..., None])
    den = jnp.sum(p, axis=-1)
    o = jnp.einsum('bdhnqk,bdhnke->bdhnqe', p, vb.astype(jnp.float32)) / den[..., None]

    def from_sub(t):
        tail = t.shape[5:]
        t = t.reshape((b, dilation, h, lp) + tail)[:, :, :, :L]
        t = jnp.moveaxis(t, 3, 1)
        return t.reshape((b, s, h) + tail)

    return from_sub(o), from_sub(m), from_sub(den)


def setup_inputs(seed: int = 0) -> dict:
    key = jax.random.key(seed)
    ks = jax.random.split(key, 16)
    f32 = jnp.float32
    x = jax.random.normal(ks[0], (BATCH, SEQ, D_MODEL), f32)
    c = jax.random.normal(ks[1], (BATCH, D_MODEL), f32)
    positions = (jnp.arange(SEQ, dtype=jnp.int32)[None, :]
                 + jax.random.randint(ks[2], (BATCH, 1), 0, MAX_POS_OFFSET, dtype=jnp.int32))
    norm_g = 1.0 + 0.05 * jax.random.normal(ks[3], (DEPTH, D_MODEL), f32)
    w_ada = 0.5 * D_MODEL ** -0.5 * jax.random.normal(ks[4], (DEPTH, D_MODEL, 3 * D_MODEL), f32)
    b_ada = 0.02 * jax.random.normal(ks[5], (DEPTH, 3 * D_MODEL), f32)
    w_in = D_MODEL ** -0.5 * jax.random.normal(ks[6], (DEPTH, D_MODEL, IN_COLS), f32)
    conv_w = CONV_SIZE ** -0.5 * jax.random.normal(ks[7], (DEPTH, CONV_SIZE, CONV_WIDTH), f32)
    conv_b = 0.02 * jax.random.normal(ks[8], (DEPTH, CONV_WIDTH), f32)
    conv_ln_g = 1.0 + 0.05 * jax.random.normal(ks[9], (DEPTH, CONV_WIDTH), f32)
    conv_ln_b = 0.02 * jax.random.normal(ks[10], (DEPTH, CONV_WIDTH), f32)
    w_conv_out = CONV_WIDTH ** -0.5 * jax.random.normal(ks[11], (DEPTH, CONV_WIDTH, D_MODEL), f32)
    w_att_out = ATT_OUT ** -0.5 * jax.random.normal(ks[12], (DEPTH, ATT_OUT, D_MODEL), f32)
    w_o = D_MODEL ** -0.5 * jax.random.normal(ks[13], (DEPTH, D_MODEL, D_MODEL), f32)
    final_g = 1.0 + 0.05 * jax.random.normal(ks[14], (D_MODEL,), f32)
    return {"x": x, "c": c, "positions": positions, "norm_g": norm_g,
            "w_ada": w_ada, "b_ada": b_ada, "w_in": w_in, "conv_w": conv_w,
            "conv_b": conv_b, "conv_ln_g": conv_ln_g, "conv_ln_b": conv_ln_b,
            "w_conv_out": w_conv_out, "w_att_out": w_att_out, "w_o": w_o,
            "final_g": final_g}


def reference(x, c, positions, norm_g, w_ada, b_ada, w_in, conv_w, conv_b, conv_ln_g,
              conv_ln_b, w_conv_out, w_att_out, w_o, final_g):
    b, s, _ = x.shape
    split_idx = np.cumsum(IN_SPLITS)[:-1].tolist()
    for layer in range(DEPTH):
        mod = c @ w_ada[layer] + b_ada[layer]
        shift, scale, gate = [t[:, None, :] for t in jnp.split(mod, 3, axis=-1)]
        h = _rmsnorm(x, norm_g[layer]) * (1.0 + scale) + shift

        proj = h @ w_in[layer]
        (glu_a, glu_b, z_conv, q, k, v, z_att, g_conv, g_att) = jnp.split(proj, split_idx, axis=-1)

        u = glu_a * jax.nn.sigmoid(glu_b)
        u = lax.conv_general_dilated(
            u, conv_w[layer][:, None, :].astype(u.dtype), window_strides=(1,),
            padding=[(CONV_SIZE - 1, 0)], dimension_numbers=('NWC', 'WIO', 'NWC'),
            feature_group_count=CONV_WIDTH) + conv_b[layer]
        u = jax.nn.silu(_layernorm(u, conv_ln_g[layer], conv_ln_b[layer]))
        y_conv = (u * jax.nn.silu(z_conv)) @ w_conv_out[layer]

        q = _partial_rope(q.reshape(b, s, N_ATT_HEADS, HEAD_DIM), positions)
        k = _partial_rope(k.reshape(b, s, N_ATT_HEADS, HEAD_DIM), positions)
        v = v.reshape(b, s, N_ATT_HEADS, HEAD_DIM)
        outs, maxes, dens = [], [], []
        for gi, (window, dilation) in enumerate(DILATION_GROUPS):
            sl = slice(gi * HEADS_PER_GROUP, (gi + 1) * HEADS_PER_GROUP)
            o_g, m_g, d_g = _dilated_window_group(q[:, :, sl], k[:, :, sl], v[:, :, sl],
                                                  window, dilation)
            outs.append(o_g); maxes.append(m_g); dens.append(d_g)
        m_all = jnp.maximum(jnp.maximum(maxes[0], maxes[1]), maxes[2])
        wts = [d_g * jnp.exp(m_g - m_all) for m_g, d_g in zip(maxes, dens)]
        w_sum = wts[0] + wts[1] + wts[2]
        att = (wts[0][..., None] * outs[0] + wts[1][..., None] * outs[1]
               + wts[2][..., None] * outs[2]) / w_sum[..., None]
        att = att.reshape(b, s, ATT_OUT).astype(x.dtype)
        y_att = (att * jax.nn.silu(z_att)) @ w_att_out[layer]

        merged = jax.nn.sigmoid(g_conv) * y_conv + jax.nn.sigmoid(g_att) * y_att
        x = x + gate * (merged @ w_o[layer])
    return _rmsnorm(x, final_g)
```

```python
import contextlib
import numpy as np
import ml_dtypes
import concourse.bass as bass
import concourse.mybir as mybir
from concourse.bass_utils import run_bass_kernel_spmd

F32 = mybir.dt.float32
BF16 = mybir.dt.bfloat16
I32 = mybir.dt.int32
AF = mybir.ActivationFunctionType
ALU = mybir.AluOpType

ENGS = ("pe", "act", "dve", "pool", "sp")


class T:
    __slots__ = ("name", "w", "r", "dsem", "dcnt", "excl")

    def __init__(self, name, excl=False):
        self.name = name
        self.excl = excl
        self.w = None
        self.r = []
        self.dsem = None
        self.dcnt = 0


class Sched:
    def __init__(self, nc):
        self.nc = nc
        self.ops = {e: [] for e in ENGS}
        self.dma_tiles = []
        import os
        self.total = 0
        self.limit = int(os.environ.get("OPLIMIT", "100000000"))

    @staticmethod
    def _deps(reads, writes):
        deps = []
        for t in reads:
            if t.w is not None:
                deps.append(t.w)
        for t in writes:
            if t.w is not None:
                deps.append(t.w)
            deps.extend(t.r)
        return deps

    def op(self, eng, fn, reads=(), writes=(), inc=True):
        self.total += 1
        if self.total > self.limit:
            return None
        writes = list(writes) + [t for t in reads if t.excl]
        reads = [t for t in reads if not t.excl]
        deps = self._deps(reads, writes)
        seq = len(self.ops[eng])
        self.ops[eng].append(dict(fn=fn, deps=deps, inc=inc, dma=None))
        me = ("e", eng, seq)
        for t in reads:
            t.r.append(me)
        for t in writes:
            t.w = me
            t.r = []
        return me

    def dma(self, eng, fn, reads=(), writes=(), key=None):
        self.total += 1
        if self.total > self.limit:
            return None
        if key is None:
            key = writes[0] if writes else reads[0]
        deps = self._deps(reads, writes)
        if key.dsem is None:
            key.dsem = True
            self.dma_tiles.append(key)
        key.dcnt += 16
        me = ("d", key, key.dcnt)
        self.ops[eng].append(dict(fn=fn, deps=deps, inc=False, dma=key))
        for t in reads:
            t.r.append(me)
        for t in writes:
            t.w = me
            t.r = []
        return me

    def wait_all(self, eng, tiles):
        deps = []
        for t in tiles:
            if t.w is not None:
                deps.append(t.w)
            deps.extend(t.r)
        self.ops[eng].append(dict(fn=None, deps=deps, inc=False, dma=None))

    def emit(self):
        nc = self.nc
        with contextlib.ExitStack() as st:
            esem = {e: st.enter_context(nc.semaphore("s_" + e)) for e in ENGS}
            for i, t in enumerate(self.dma_tiles):
                t.dsem = st.enter_context(nc.semaphore("d%d" % i))
            need = {}
            for e in ENGS:
                ops = self.ops[e]
                n = len(ops)
                c = 0
                incs = []
                for o in ops:
                    if o["inc"]:
                        c += 1
                    incs.append(c)
                nd = [None] * n
                nxt = None
                for i in range(n - 1, -1, -1):
                    if ops[i]["inc"]:
                        nxt = incs[i]
                    nd[i] = nxt
                need[e] = nd
            block = st.enter_context(nc.Block())

            def make(e):
                def body(engh):
                    waited = {}
                    for i, o in enumerate(self.ops[e]):
                        req = {}
                        for d in o["deps"]:
                            if d[0] == "e":
                                _, se, sq = d
                                if se == e and e == "pe":
                                    continue
                                v = need[se][sq]
                                if v is None and self.limit < 100000000:
                                    continue
                                assert v is not None, ("dep on non-inc tail op", se, sq)
                                if se == e:
                                    assert sq < i
                                k = ("e", se)
                                sem = esem[se]
                            else:
                                _, kt, v = d
                                k = ("d", id(kt))
                                sem = kt.dsem
                            if waited.get(k, 0) >= v:
                                continue
                            if k not in req or req[k][1] < v:
                                req[k] = (sem, v)
                        for k, (sem, v) in req.items():
                            engh.wait_ge(sem, v)
                            waited[k] = v
                        if o["fn"] is None:
                            continue
                        ins = o["fn"](engh)
                        if o["dma"] is not None:
                            ins.then_inc(o["dma"].dsem, 16)
                        elif o["inc"]:
                            ins.then_inc(esem[e], 1)
                return body

            block.tensor(make("pe"))
            block.scalar(make("act"))
            block.vector(make("dve"))
            block.gpsimd(make("pool"))
            block.sync(make("sp"))


D = 1024
NCH = 8
SBT = 2048
NSB = 3
TOKL = NSB * SBT
OWN = 4096
DIL = (1, 4, 16)
RSL = (3, 6, 18)
EPS = 1e-6
COL_GA, COL_GB, COL_ZC, COL_Q, COL_K, COL_V, COL_ZA, COL_GC, COL_GT = 0, 1024, 2048, 3072, 4608, 6144, 7680, 8192, 9216
IN_COLS = 10240
NEG = -30000.0
ARENA_BYTES = 62464
NWB = 4
NTAP_PE = 27
HALO_BLOCKS = {0: [15], 1: [12, 13, 14, 15], 2: list(range(16))}
INV_FREQ = (np.float32(500000.0) ** (-(np.arange(8, dtype=np.float32) * np.float32(2.0) / np.float32(16.0)))).astype(np.float32)


def build_program(debug=False):
    nc = bass.Bass("TRN2", target_bir_lowering=False)
    dt_in = lambda name, shape, dt=F32: nc.dram_tensor(name, list(shape), dt, kind="ExternalInput").ap()
    x_d = dt_in("x", [TOKL, D])
    pos_d = dt_in("pos", [TOKL], I32)
    vecs_d = dt_in("vecs", [40, 128])
    cw_d = dt_in("cw", [248, 128])
    bada_d = dt_in("b_ada", [1, 3 * D])
    wada_d = dt_in("w_ada", [D, 3 * D])
    win_d = dt_in("w_in", [D, IN_COLS])
    wco_d = dt_in("w_conv_out", [D, D])
    wao_d = dt_in("w_att_out", [512, D])
    wo_d = dt_in("w_o", [D, D])
    fg_d = dt_in("final_g", [1, D])
    mask_d = dt_in("masks", [128, 512])
    hflag_d = dt_in("hflag", [128, 1])
    out_d = nc.dram_tensor("out", [OWN, D], F32, kind="ExternalOutput").ap()
    dscr = nc.dram_tensor("diag_scr", [8, 128, 31 * 128], BF16).ap()
    t_dscr = T("dscr")

    S = Sched(nc)
    A = nc.alloc_sbuf_tensor

    hT = A("hT", [128, NCH, SBT], BF16); t_hT = T("hT")
    KR = {}; VR = {}; t_KR = {}; t_VR = {}
    for g in range(3):
        for hf in range(2):
            KR[g, hf] = A("KR%d%d" % (g, hf), [128, RSL[g], 2, 128], BF16)
            VR[g, hf] = A("VR%d%d" % (g, hf), [128, RSL[g], 4, 64], BF16)
            t_KR[g, hf] = [T("KR%d%d_%d" % (g, hf, s)) for s in range(RSL[g])]
            t_VR[g, hf] = [T("VR%d%d_%d" % (g, hf, s)) for s in range(RSL[g])]
    agT = A("agT", [128, 4, SBT], BF16); t_agT = [T("agT%d" % p) for p in range(4)]
    wog = A("wog", [128, NCH, D], BF16); t_wog = T("wog")
    fgb = A("fgb", [128, D], F32); t_fgb = T("fgb")
    tcos = A("tcos", [128, 48, 8], F32); tsin = A("tsin", [128, 48, 8], F32); t_tab = T("tab")
    ident = A("ident", [128, 128], BF16); identf = A("identf", [128, 128], F32); t_id = T("ident")
    ones64 = A("ones64", [128, 64], BF16); onesdiv = A("onesdiv", [128, 128], BF16)
    onesf = A("onesf", [1, 128], F32); t_ones = T("ones")
    maskb = A("maskb", [128, 512], BF16); t_mask = T("mask")
    hflag = A("hflag_s", [128, 1], F32); t_hflag = T("hflag")
    vT = A("vT", [128, 40], F32); t_vT = T("vT")
    cwh = A("cwh", [128, 248], F32); t_cwh = T("cwh")
    modT = A("modT", [128, 16], F32); gmod = A("gmod", [128, 8], F32); t_mod = T("mod")
    ss = A("ss", [128, 2], F32); sd = A("sd", [128, 2], F32); rs2 = A("rs2", [128, 2], F32)
    t_ss = T("ss")
    fsc = A("fsc", [128, 8], F32); t_fsc = T("fsc")
    wbuf = [A("wbuf%d" % i, [128, NCH, 128], BF16) for i in range(NWB)]
    t_wbuf = [T("wbuf%d" % i) for i in range(NWB)]
    xn2 = A("xn2", [128, 2, D], F32); t_xn = [T("xn0"), T("xn1")]
    uhist = A("uhist", [128, NCH, 32], BF16); t_uhist = [T("uhist%d" % c) for c in range(NCH)]
    arena = A("arena", [128, ARENA_BYTES // 4], F32)

    def carve(off, nbytes, dt, pattern=None, **kw):
        assert off % 4 == 0 and nbytes % 4 == 0 and off + nbytes <= ARENA_BYTES
        v = arena[:, off // 4:(off + nbytes) // 4]
        if dt != F32:
            v = v.bitcast(dt)
        if pattern:
            v = v.rearrange(pattern, **kw)
        return v

    su_wada = [carve(i * 16384, 16384, F32, "p (c n) -> p c n", c=8) for i in range(2)]
    t_su_wada = [T("su_wada0"), T("su_wada1")]
    su_modrow = carve(32768, 12288, F32)[0:1, :]; t_su_modrow = T("su_modrow")
    su_gate = carve(45056, 4096, F32); t_su_gate = T("su_gate")
    su_wo = carve(49152, 8192, F32, "p (c n) -> p c n", c=8); t_su_wo = T("su_wo")
    su_vecs = carve(57344, 512, F32)[0:40, :]; t_su_vecs = T("su_vecs")
    su_cw = [carve(57856 + i * 512, 512, F32)[0:124, :] for i in range(2)]; t_su_cw = T("su_cw")
    su_bada = carve(58880, 4096 * 3, F32)[0:1, :] if False else None
    su_maskf = carve(58880, 2048, F32); t_su_maskf = T("su_maskf")
    su_one = carve(60928, 4, F32)[0:1, :]; t_su_one = T("su_one")
    su_diag = [hT[:, 2 * i:2 * i + 2, :].rearrange("p a t -> p (a t)")[:, 0:3968].rearrange("p (j n) -> p j n", j=31) for i in range(2)]
    t_su_diag = [T("su_diag0"), T("su_diag1")]
    SETUP_T = t_su_diag + [t_hT, t_su_modrow, t_su_gate, t_su_wo, t_su_vecs, t_su_cw, t_su_maskf, t_su_one] + t_su_wada
    x_xb = [carve(i * 2048, 2048, BF16) for i in range(2)]; t_x_xb = [T("x_xb0"), T("x_xb1")]
    x_sq = carve(4096, 2048, BF16); t_x_sq = T("x_sq")
    x_posi = carve(6144, 48 * 4, I32); x_posf = carve(6400, 48 * 4, F32)
    x_ang = carve(8192, 1536, F32, "p (b j) -> p b j", j=8)
    x_y = [carve(9728 + i * 1536, 1536, F32) for i in range(2)]
    x_ki = carve(12800, 1536, I32); x_kf = carve(14336, 1536, F32); x_m = carve(15872, 1536, F32)
    t_x_tab = T("x_tabtmp")
    X_T = t_x_xb + [t_x_sq, t_x_tab]
    a_accn = carve(0, 16384, F32, "p (a t) -> p a t", a=2); a_accd = carve(16384, 16384, F32, "p (a t) -> p a t", a=2)
    t_a_acc = T("a_acc")
    a_wq = [carve(32768 + i * 4096, 4096, BF16, "p (c n) -> p c n", c=8) for i in range(5)]
    t_a_wq = [T("a_wq%d" % i) for i in range(5)]
    a_qk = [carve(53248 + i * 1024, 1024, BF16) for i in range(2)]
    t_a_qk = [T("a_qk0"), T("a_qk1")]; t_a_qkr = [T("a_qkr0"), T("a_qkr1")]
    a_QT = [carve(55296 + i * 1024, 1024, BF16, "p (a h t) -> p a h t", a=2, h=2) for i in range(2)]; t_a_QT = [T("a_QT0"), T("a_QT1")]
    a_PT = [carve(57344 + i * 2048, 2048, BF16, "p (h t) -> p h t", h=4) for i in range(2)]
    t_a_PT = [[T("a_PT%d%d" % (i, p)) for p in range(2)] for i in range(2)]
    a_rt = carve(61440, 1024, F32, "p (k h j) -> p k h j", k=4, h=8); t_a_rt = [T("a_rt%d" % k) for k in range(4)]
    a_rd = carve(53248, 2048, F32); a_sz = carve(55296, 2048, F32); a_ft = carve(57344, 2048, F32)
    t_a_fin = T("a_fin")
    ATT_T = [t_a_acc, t_a_fin] + t_a_rt + t_a_wq + t_a_qk + t_a_qkr + t_a_QT + t_a_PT[0] + t_a_PT[1]
    c_cv = carve(0, 16384, F32, "p (c t) -> p c t", c=8); t_c_cv = [T("c_cv%d" % c) for c in range(8)]
    UW = 544
    c_uT = carve(16384, 8 * UW * 2, BF16, "p (c t) -> p c t", c=8); t_c_uT = [T("c_uT%d" % c) for c in range(8)]
    c_ucT = [carve(c * 2048, 1024, BF16) for c in range(8)]; t_c_ucT = [T("c_ucT%d" % c) for c in range(8)]
    c_mgT = [carve(16384 + n * 1024, 1024, BF16) for n in range(8)]; t_c_mgT = [T("c_mgT%d" % c) for c in range(8)]
    c_diag = [carve(25088 + i * 8192, 7936, BF16, "p (j n) -> p j n", j=31) for i in range(2)]; t_c_diag = [T("c_diag0"), T("c_diag1")]
    c_cvb = [carve(41472 + i * 1024, 1024, BF16) for i in range(2)]; t_c_cvb = [T("c_cvb0"), T("c_cvb1")]
    c_sqb = [carve(43520 + i * 1024, 1024, BF16) for i in range(2)]; t_c_sqb = [T("c_sqb0"), T("c_sqb1")]
    c_th = [carve(45568 + i * 2048, 2048, F32) for i in range(2)]; t_c_th = [T("c_th0"), T("c_th1")]
    c_var = carve(49664, 2048, F32); c_rstd = carve(51712, 2048, F32); c_nmr = carve(53760, 2048, F32)
    t_c_ln = T("c_ln")
    c_lt = carve(55808, 2048, F32); t_c_lt = T("c_lt")
    c_szc = carve(57856, 2048, F32); t_c_szc = T("c_szc")
    c_sl = carve(59904, 2048, F32); t_c_sl = T("c_sl")
    c_m2 = c_lt; t_c_m2 = t_c_lt
    c_ot = carve(49664, 8192, F32, "p (s n) -> p s n", s=2); t_c_ot = [T("c_ot0"), T("c_ot1")]
    CONV_T = (t_c_ot + t_c_cv + t_c_uT + t_c_ucT + t_c_mgT + t_c_cvb + t_c_sqb + t_c_th + t_c_diag
              + [t_c_ln, t_c_lt, t_c_szc, t_c_sl])

    PB = [nc.alloc_psum_tensor("pb%d" % i, [128, 512], F32) for i in range(8)]
    t_PB = [T("pb%d" % i, excl=True) for i in range(8)]
    PBh = [PB[i][:].bitcast(BF16) for i in range(8)]
    t_pqk = [t_PB[0], t_PB[1]]; t_pv = [t_PB[2], t_PB[2]]; t_ptr = [t_PB[3], t_PB[3]]
    t_pss = [t_PB[4], t_PB[5]]; t_pso = [t_PB[6], t_PB[7]]
    ATT_P = []
    scratch = A("fence_scratch", [128, 1], F32); t_scr = T("scr")

    state = {"phase_T": SETUP_T + t_PB}

    def switch_phase(new_T):
        old = state["phase_T"]
        S.op("dve", lambda e: e.memset(scratch[:], 0.0), writes=list(dict.fromkeys(list(old) + list(new_T) + [t_scr])))
        state["phase_T"] = list(new_T)

    wstate = {"i": 0}

    def load_chunk(src_ap, kch=NCH):
        i = wstate["i"] % NWB
        wstate["i"] += 1
        b, tb = wbuf[i], t_wbuf[i]
        S.dma("pool", lambda e, b=b, s=src_ap, k=kch: e.dma_start(out=b[:, 0:k, :], in_=s.rearrange("(c p) n -> p c n", p=128)),
              writes=[tb])
        return b, tb

    import os as _os
    sstop = int(_os.environ.get('SSTOP', '99'))
    def setup():
        S.dma("sp", lambda e: e.dma_start(out=su_vecs, in_=vecs_d[:, :]), writes=[t_su_vecs])
        for i in range(2):
            S.dma("sp", lambda e, i=i: e.dma_start(out=su_cw[i], in_=cw_d[i * 124:(i + 1) * 124, :]), writes=[t_su_cw])
        S.dma("sp", lambda e: e.dma_start(out=su_modrow, in_=bada_d[:, :]), writes=[t_su_modrow])
        S.dma("sp", lambda e: e.dma_start(out=su_maskf, in_=mask_d[:, :]), writes=[t_su_maskf])
        S.dma("sp", lambda e: e.dma_start(out=hflag[:], in_=hflag_d[:, :]), writes=[t_hflag])
        S.dma("sp", lambda e: e.dma_start(out=fgb[:], in_=fg_d.partition_broadcast(128)), writes=[t_fgb])
        if sstop < 2: return
        S.op("dve", lambda e: e.memset(identf[:], 0.0), writes=[t_id])
        S.op("pool", lambda e: e.affine_select(out=identf[:], in_=identf[:], pattern=[[-1, 128]], compare_op=ALU.not_equal,
                                               fill=1.0, base=0, channel_multiplier=1), reads=[t_id], writes=[t_id])
        S.op("dve", lambda e: e.tensor_copy(out=ident[:], in_=identf[:]), reads=[t_id], writes=[t_id])
        S.op("dve", lambda e: e.memset(ones64[:], 1.0), writes=[t_ones])
        S.op("dve", lambda e: e.memset(onesdiv[:], 1.0 / 1024.0), writes=[t_ones])
        S.op("dve", lambda e: e.memset(onesf[:], 1.0), writes=[t_ones])
        S.op("dve", lambda e: e.memset(su_one, 1.0), writes=[t_su_one])
        S.op("dve", lambda e: e.tensor_scalar(out=maskb[:], in0=su_maskf, scalar1=-1.0, scalar2=None, op0=ALU.is_ge), reads=[t_su_maskf], writes=[t_mask])
        if sstop < 3: return
        S.op("pe", lambda e: e.transpose(PB[0][:, 0:40], su_vecs, identf[0:40, 0:40]), reads=[t_su_vecs, t_id], writes=[t_PB[0]])
        S.op("dve", lambda e: e.tensor_copy(out=vT[:], in_=PB[0][:, 0:40]), reads=[t_PB[0]], writes=[t_vT])
        for i in range(2):
            S.op("pe", lambda e, i=i: e.transpose(PB[1][:, i * 124:(i + 1) * 124], su_cw[i], identf[0:124, 0:124]),
                 reads=[t_su_cw, t_id], writes=[t_PB[1]])
        S.op("dve", lambda e: e.tensor_scalar(out=cwh[:], in0=PB[1][:, 0:248], scalar1=0.5, scalar2=None, op0=ALU.mult),
             reads=[t_PB[1]], writes=[t_cwh])
        cw3s = cwh[:].rearrange("p (j c) -> p j c", c=8)
        for c in range(NCH):
            sdg, tsd = su_diag[c % 2], t_su_diag[c % 2]
            S.op("dve", lambda e, sdg=sdg, c=c: e.tensor_tensor(out=sdg, in0=ident[:].unsqueeze(1).broadcast_to([128, 31, 128]),
                                                               in1=cw3s[:, :, c].unsqueeze(2).broadcast_to([128, 31, 128]), op=ALU.mult),
                 reads=[t_id, t_cwh], writes=[tsd])
            S.dma("sp", lambda e, sdg=sdg, c=c: e.dma_start(out=dscr[c].rearrange("p (j n) -> p j n", j=31), in_=sdg), reads=[tsd], writes=[t_dscr])
        if sstop < 4: return
        for cb in range(6):
            wb_, twb_ = su_wada[cb % 2], t_su_wada[cb % 2]
            S.dma("sp", lambda e, wb_=wb_, cb=cb: e.dma_start(
                out=wb_, in_=wada_d[:, cb * 512:(cb + 1) * 512].rearrange("(c p) n -> p c n", p=128)), writes=[twb_])
            pbank = 2 + (cb % 2)
            for k in range(NCH):
                S.op("pe", lambda e, wb_=wb_, k=k, pbank=pbank: e.matmul(PB[pbank][0:1, :], lhsT=vT[:, 32 + k:33 + k], rhs=wb_[:, k, :],
                                                                        start=(k == 0), stop=(k == NCH - 1)),
                     reads=[twb_, t_vT], writes=[t_PB[pbank]], inc=(k == NCH - 1))
            S.op("dve", lambda e, cb=cb, pbank=pbank: e.tensor_tensor(out=su_modrow[:, cb * 512:(cb + 1) * 512], in0=PB[pbank][0:1, :],
                                                                      in1=su_modrow[:, cb * 512:(cb + 1) * 512], op=ALU.add),
                 reads=[t_PB[pbank], t_su_modrow], writes=[t_su_modrow])
        if sstop < 5: return
        for j in range(16):
            S.op("pe", lambda e, j=j: e.matmul(PB[4][:, j:j + 1], lhsT=su_modrow[:, j * 128:(j + 1) * 128], rhs=su_one,
                                               start=True, stop=True),
                 reads=[t_su_modrow, t_su_one], writes=[t_PB[4]], inc=(j == 15))
        S.op("dve", lambda e: e.tensor_copy(out=modT[:], in_=PB[4][:, 0:16]), reads=[t_PB[4]], writes=[t_mod])
        S.op("dve", lambda e: e.scalar_tensor_tensor(out=gmod[:], in0=modT[:, 8:16], scalar=1.0, in1=vT[:, 0:8],
                                                     op0=ALU.add, op1=ALU.mult), reads=[t_mod, t_vT], writes=[t_mod])
        for hh in range(2):
            S.op("pe", lambda e, hh=hh: e.matmul(PB[5 + hh][:, :], lhsT=onesf[:, :], rhs=su_modrow[:, 2048 + hh * 512:2048 + (hh + 1) * 512],
                                                 start=True, stop=True), reads=[t_ones, t_su_modrow], writes=[t_PB[5 + hh]])
            S.op("act", lambda e, hh=hh: e.activation(out=su_gate[:, hh * 512:(hh + 1) * 512], in_=PB[5 + hh][:, :], func=AF.Copy, scale=0.5),
                 reads=[t_PB[5 + hh]], writes=[t_su_gate])
        if sstop < 6: return
        for cb in range(4):
            S.dma("sp", lambda e, cb=cb: e.dma_start(out=su_wo, in_=wo_d[:, cb * 256:(cb + 1) * 256].rearrange("(c p) n -> p c n", p=128)),
                  writes=[t_su_wo])
            S.op("dve", lambda e, cb=cb: e.tensor_tensor(out=wog[:, :, cb * 256:(cb + 1) * 256], in0=su_wo,
                                                         in1=su_gate[:, cb * 256:(cb + 1) * 256].unsqueeze(1).broadcast_to([128, NCH, 256]),
                                                         op=ALU.mult), reads=[t_su_wo, t_su_gate], writes=[t_wog])

    setup()
    def phase_x(SB):
        switch_phase(X_T + t_PB)
        tabq = []

        def TABQ(fn, *a_, **k_):
            tabq.append((fn, a_, k_))

        def tab_emit(n):
            for _ in range(n):
                if tabq:
                    fn, a_, k_ = tabq.pop(0)
                    fn(*a_, **k_)

        for g in range(3):
            d = DIL[g]
            src = pos_d[SB * SBT:(SB + 1) * SBT].rearrange("(n i r) -> i n r", i=128, r=d)
            dst = x_posi[:, g * 16:(g + 1) * 16].rearrange("p (n r) -> p n r", r=d)
            TABQ(S.dma, "sp", lambda e, src=src, dst=dst: e.dma_start(out=dst, in_=src, allow_slow_non_contiguous=True), writes=[t_x_tab])
        TABQ(S.op, "dve", lambda e: e.tensor_copy(out=x_posf, in_=x_posi), reads=[t_x_tab], writes=[t_x_tab])
        for j in range(8):
            TABQ(S.op, "dve", lambda e, j=j: e.tensor_scalar(out=x_ang[:, :, j], in0=x_posf, scalar1=float(INV_FREQ[j]), scalar2=None,
                                                       op0=ALU.mult), reads=[t_x_tab], writes=[t_x_tab])
        angf = x_ang.rearrange("p b j -> p (b j)")
        for which in range(2):
            y = x_y[which]
            TABQ(S.op, "dve", lambda e, y=y, which=which: e.tensor_scalar(out=y, in0=angf, scalar1=float(1.0 / (2.0 * np.pi)),
                                                                    scalar2=0.25 * which, op0=ALU.mult, op1=ALU.add),
                 reads=[t_x_tab], writes=[t_x_tab])
            TABQ(S.op, "dve", lambda e, y=y: e.tensor_copy(out=x_ki, in_=y), reads=[t_x_tab], writes=[t_x_tab])
            TABQ(S.op, "dve", lambda e: e.tensor_copy(out=x_kf, in_=x_ki), reads=[t_x_tab], writes=[t_x_tab])
            TABQ(S.op, "dve", lambda e, y=y: e.tensor_tensor(out=y, in0=y, in1=x_kf, op=ALU.subtract), reads=[t_x_tab], writes=[t_x_tab])
            TABQ(S.op, "dve", lambda e, y=y: e.tensor_scalar(out=x_m, in0=y, scalar1=0.5, scalar2=None, op0=ALU.is_gt),
                 reads=[t_x_tab], writes=[t_x_tab])
            TABQ(S.op, "dve", lambda e, y=y: e.tensor_tensor(out=y, in0=y, in1=x_m, op=ALU.subtract), reads=[t_x_tab], writes=[t_x_tab])
            TABQ(S.op, "dve", lambda e, y=y: e.tensor_scalar(out=x_m, in0=y, scalar1=-0.5, scalar2=None, op0=ALU.is_lt),
                 reads=[t_x_tab], writes=[t_x_tab])
            TABQ(S.op, "dve", lambda e, y=y: e.tensor_tensor(out=y, in0=y, in1=x_m, op=ALU.add), reads=[t_x_tab], writes=[t_x_tab])
            tab = (tsin, tcos)[which]
            TABQ(S.op, "act", lambda e, y=y, tab=tab: e.activation(out=tab[:].rearrange("p b j -> p (b j)"), in_=y, func=AF.Sin, scale=6.283185),
                 reads=[t_x_tab], writes=[t_tab])
        def evac_pair(j0, bank0):
            for s in range(2):
                j = j0 + s
                pb = bank0 + s
                for c in range(NCH):
                    dst = hT[:, c, j * 128:(j + 1) * 128]
                    src = PBh[pb][:, c * 128:(c + 1) * 128]
                    if s == 0:
                        S.op("act", lambda e, dst=dst, src=src, c=c: e.activation(out=dst, in_=src, func=AF.Identity, scale=gmod[:, c:c + 1],
                                                                                  bias=modT[:, c:c + 1]),
                             reads=[t_PB[pb], t_mod], writes=[t_hT])
                    else:
                        S.op("dve", lambda e, dst=dst, src=src, c=c: e.tensor_scalar(out=dst, in0=src, scalar1=gmod[:, c:c + 1], scalar2=modT[:, c:c + 1],
                                                                                    op0=ALU.mult, op1=ALU.add),
                             reads=[t_PB[pb], t_mod], writes=[t_hT])

        prev = None
        for pi, j0 in enumerate(range(0, 16, 2)):
            bank0 = 2 * (pi % 2)
            for s in range(2):
                row0 = (SB * 16 + j0 + s) * 128
                S.dma("sp", lambda e, s=s, row0=row0: e.dma_start(out=xn2[:, s, :], in_=x_d[row0:row0 + 128, :]), writes=[t_xn[s]])
            for s in range(2):
                S.op("act", lambda e, s=s: e.activation(out=x_sq, in_=xn2[:, s, :], func=AF.Square, accum_out=ss[:, s:s + 1]),
                     reads=[t_xn[s]], writes=[t_x_sq, t_ss])
            S.op("act", lambda e: e.activation(out=sd[:], in_=ss[:], func=AF.Sqrt, scale=1.0 / 1024.0, bias=EPS), reads=[t_ss], writes=[t_ss])
            S.op("dve", lambda e: e.reciprocal(out=rs2[:], in_=sd[:]), reads=[t_ss], writes=[t_ss])
            for s in range(2):
                pb = bank0 + s
                S.op("dve", lambda e, s=s: e.tensor_scalar(out=x_xb[s], in0=xn2[:, s, :], scalar1=rs2[:, s:s + 1], scalar2=None, op0=ALU.mult),
                     reads=[t_xn[s], t_ss], writes=[t_x_xb[s]])
                for c in range(NCH):
                    S.op("pe", lambda e, s=s, c=c, pb=pb: e.transpose(PBh[pb][:, c * 128:(c + 1) * 128], x_xb[s][:, c * 128:(c + 1) * 128], ident[:]),
                         reads=[t_x_xb[s], t_id], writes=[t_PB[pb]], inc=(c == NCH - 1))
            if prev is not None:
                evac_pair(*prev)
            prev = (j0, bank0)
            tab_emit(6)
        evac_pair(*prev)
        tab_emit(1000)

    wq_state = {"i": 0}

    def attention(SB, half):
        own = SB >= 1
        ctxs = []
        bi = 0
        for g in (2, 1, 0):
            blocks = list(range(16)) if own else HALO_BLOCKS[g]
            for k_, lb in enumerate(blocks):
                d = DIL[g]
                n, r = divmod(lb, d)
                gb = SB * 16 + lb
                tok0 = 128 * n * d + r
                ctxs.append(dict(g=g, d=d, lb=lb, gb=gb, par=bi % 2, first=(k_ == 0),
                                 tsl=slice(tok0, tok0 + 127 * d + 1, d), slot=gb % RSL[g], pslot=(gb - d) % RSL[g]))
                bi += 1
        gparts = {}

        def st_P(cx, which=("q", "k", "v")):
            g, par, tsl = cx["g"], cx["par"], cx["tsl"]
            if cx["first"] and g not in gparts:
                h0 = 8 * g + 4 * half
                parts = {}
                for name, base in (("q", COL_Q), ("k", COL_K), ("v", COL_V)):
                    if name == "q" and not own:
                        continue
                    i = wq_state["i"] % 5
                    wq_state["i"] += 1
                    col0 = base + h0 * 64
                    S.dma("pool", lambda e, i=i, col0=col0: e.dma_start(
                        out=a_wq[i], in_=win_d[:, col0:col0 + 256].rearrange("(c p) n -> p c n", p=128)), writes=[t_a_wq[i]])
                    parts[name] = (a_wq[i], t_a_wq[i])
                gparts[g] = parts
            parts = gparts[g]
            pqk = PB[par]
            pv = PB[2][:, par * 256:(par + 1) * 256]
            for name in which:
                if name not in parts:
                    continue
                wt, twt = parts[name]
                if name == "q":
                    o, to = pqk[:, 0:256], t_pqk[par]
                elif name == "k":
                    o, to = pqk[:, 256:512], t_pqk[par]
                else:
                    o, to = pv, t_pv[par]
                for c in range(NCH):
                    S.op("pe", lambda e, o=o, wt=wt, c=c, tsl=tsl: e.matmul(o, lhsT=hT[:, c, tsl], rhs=wt[:, c, :],
                                                                          start=(c == 0), stop=(c == NCH - 1)),
                         reads=[t_hT, twt], writes=[to], inc=(c == NCH - 1))

        def st_EP(cx):
            g, par, slot, lb = cx["g"], cx["par"], cx["slot"], cx["lb"]
            vr, tvr = VR[g, half], t_VR[g, half]
            pqk = PB[par]
            pv = PB[2][:, par * 256:(par + 1) * 256]
            S.op("act", lambda e, vr=vr, slot=slot, pv=pv: e.activation(out=vr[:, slot, :, :], in_=pv.rearrange("p (h e) -> p h e", e=64), func=AF.Copy),
                 reads=[t_pv[par]], writes=[tvr[slot]])
            hs = slice(0, 8) if own else slice(4, 8)
            nh = 8 if own else 4
            P3 = pqk.rearrange("p (h e) -> p h e", e=64)
            Q3 = a_qk[par].rearrange("p (h e) -> p h e", e=64)
            S.op("act", lambda e, P3=P3, Q3=Q3, hs=hs: e.activation(out=Q3[:, hs, 16:64], in_=P3[:, hs, 16:64], func=AF.Copy),
                 reads=[t_pqk[par]], writes=[t_a_qk[par]])
            tb = g * 16 + lb
            cosb = tcos[:, tb, :].unsqueeze(1).broadcast_to([128, nh, 8])
            sinb = tsin[:, tb, :].unsqueeze(1).broadcast_to([128, nh, 8])
            t1, t2 = P3[:, hs, 0:8], P3[:, hs, 8:16]
            rt = [a_rt[:, k, 0:nh, :] for k in range(4)]
            for k, (ta_, tb_) in enumerate(((t1, cosb), (t2, sinb), (t2, cosb), (t1, sinb))):
                S.op("dve", lambda e, ta_=ta_, tb_=tb_, k=k, rt=rt: e.tensor_tensor(out=rt[k], in0=ta_, in1=tb_, op=ALU.mult),
                     reads=[t_pqk[par], t_tab], writes=[t_a_rt[k]])
            S.op("dve", lambda e, Q3=Q3, hs=hs, rt=rt: e.tensor_tensor(out=Q3[:, hs, 0:8], in0=rt[0], in1=rt[1], op=ALU.subtract),
                 reads=[t_a_rt[0], t_a_rt[1]], writes=[t_a_qkr[par]])
            S.op("dve", lambda e, Q3=Q3, hs=hs, rt=rt: e.tensor_tensor(out=Q3[:, hs, 8:16], in0=rt[2], in1=rt[3], op=ALU.add),
                 reads=[t_a_rt[2], t_a_rt[3]], writes=[t_a_qkr[par]])

        def st_TR(cx):
            g, par, slot = cx["g"], cx["par"], cx["slot"]
            kr, tkr = KR[g, half], t_KR[g, half]
            ptr = PBh[3][:, par * 512:(par + 1) * 512].rearrange("p (j t) -> p j t", j=4)
            js = range(4) if own else range(2, 4)
            for j in js:
                S.op("pe", lambda e, ptr=ptr, j=j, par=par: e.transpose(ptr[:, j, :], a_qk[par][:, j * 128:(j + 1) * 128], ident[:]),
                     reads=[t_a_qk[par], t_a_qkr[par], t_id], writes=[t_ptr[par]], inc=(j == 3))
            S.op("act", lambda e, ptr=ptr, kr=kr, slot=slot: e.activation(out=kr[:, slot, :, :], in_=ptr[:, 2:4, :], func=AF.Copy),
                 reads=[t_ptr[par]], writes=[tkr[slot]])
            if own:
                S.op("dve", lambda e, ptr=ptr, par=par: e.tensor_copy(out=a_QT[par][0:64, :, 0, :], in_=ptr[0:64, 0:2, :]), reads=[t_ptr[par]], writes=[t_a_QT[par]])
                S.op("dve", lambda e, ptr=ptr, par=par: e.tensor_copy(out=a_QT[par][64:128, :, 1, :], in_=ptr[64:128, 0:2, :]), reads=[t_ptr[par]], writes=[t_a_QT[par]])

        def st_QK(cx):
            g, par, slot, pslot, gb, d = cx["g"], cx["par"], cx["slot"], cx["pslot"], cx["gb"], cx["d"]
            kr, tkr = KR[g, half], t_KR[g, half]
            mk = maskb[:, 256:512] if (gb - d) < 16 else maskb[:, 0:256]
            for p in range(2):
                pss = PB[4 + p]
                for hh in range(2):
                    S.op("pe", lambda e, pss=pss, hh=hh, p=p, pslot=pslot, kr=kr, par=par: e.matmul(
                        pss[:, hh * 256:hh * 256 + 128], lhsT=kr[:, pslot, p, :], rhs=a_QT[par][:, p, hh, :], start=True, stop=True),
                        reads=[tkr[pslot], t_a_QT[par]], writes=[t_pss[p]], inc=False)
                    S.op("pe", lambda e, pss=pss, hh=hh, p=p, slot=slot, kr=kr, par=par: e.matmul(
                        pss[:, hh * 256 + 128:hh * 256 + 256], lhsT=kr[:, slot, p, :], rhs=a_QT[par][:, p, hh, :], start=True, stop=True),
                        reads=[tkr[slot], t_a_QT[par]], writes=[t_pss[p]], inc=(hh == 1))
                ptv = a_PT[par][:, 2 * p:2 * p + 2, :]
                S.op("act", lambda e, pss=pss, ptv=ptv: e.activation(out=ptv.rearrange("p h t -> p (h t)"), in_=pss[:, :], func=AF.Exp, scale=0.125),
                     reads=[t_pss[p]], writes=[t_a_PT[par][p]])
                S.op("dve", lambda e, ptv=ptv, mk=mk: e.tensor_tensor(out=ptv, in0=ptv, in1=mk.unsqueeze(1).broadcast_to([128, 2, 256]), op=ALU.mult),
                     reads=[t_mask], writes=[t_a_PT[par][p]])

        def st_PV(cx):
            g, par, slot, pslot, tsl = cx["g"], cx["par"], cx["slot"], cx["pslot"], cx["tsl"]
            vr, tvr = VR[g, half], t_VR[g, half]
            pso = PB[6 + par][:, :].rearrange("p (k a t) -> p k a t", k=2, a=2)
            for p in range(2):
                for hh in range(2):
                    h4 = 2 * p + hh
                    rows = slice(hh * 64, hh * 64 + 64)
                    tp = (0, 64) if hh else None
                    last = (p == 1 and hh == 1)
                    S.op("pe", lambda e, pso=pso, rows=rows, p=p, h4=h4, tp=tp, vr=vr, pslot=pslot, par=par: e.matmul(
                        pso[rows, 0, p, :], lhsT=vr[:, pslot, h4, :], rhs=a_PT[par][:, h4, 0:128], start=True, stop=False, tile_position=tp),
                        reads=[tvr[pslot], t_a_PT[par][p]], writes=[t_pso[par]], inc=False)
                    S.op("pe", lambda e, pso=pso, rows=rows, p=p, h4=h4, tp=tp, vr=vr, slot=slot, par=par: e.matmul(
                        pso[rows, 0, p, :], lhsT=vr[:, slot, h4, :], rhs=a_PT[par][:, h4, 128:256], start=False, stop=True, tile_position=tp),
                        reads=[tvr[slot], t_a_PT[par][p]], writes=[t_pso[par]], inc=False)
                    S.op("pe", lambda e, pso=pso, rows=rows, p=p, h4=h4, tp=tp, par=par: e.matmul(
                        pso[rows, 1, p, :], lhsT=ones64[:], rhs=a_PT[par][:, h4, 0:128], start=True, stop=False, tile_position=tp),
                        reads=[t_ones, t_a_PT[par][p]], writes=[t_pso[par]], inc=False)
                    S.op("pe", lambda e, pso=pso, rows=rows, p=p, h4=h4, tp=tp, par=par: e.matmul(
                        pso[rows, 1, p, :], lhsT=ones64[:], rhs=a_PT[par][:, h4, 128:256], start=False, stop=True, tile_position=tp),
                        reads=[t_ones, t_a_PT[par][p]], writes=[t_pso[par]], inc=last)
        def st_ACC(cx):
            g, par, tsl = cx["g"], cx["par"], cx["tsl"]
            pso = PB[6 + par][:, :].rearrange("p (k a t) -> p k a t", k=2, a=2)
            an, ad = a_accn[:, :, tsl], a_accd[:, :, tsl]
            if g == 2:
                S.op("dve", lambda e, an=an, pso=pso: e.tensor_copy(out=an, in_=pso[:, 0, :, :]), reads=[t_pso[par]], writes=[t_a_acc])
                S.op("dve", lambda e, ad=ad, pso=pso: e.tensor_copy(out=ad, in_=pso[:, 1, :, :]), reads=[t_pso[par]], writes=[t_a_acc])
            else:
                S.op("dve", lambda e, an=an, pso=pso: e.tensor_tensor(out=an, in0=pso[:, 0, :, :], in1=an, op=ALU.add),
                     reads=[t_pso[par], t_a_acc], writes=[t_a_acc])
                S.op("dve", lambda e, ad=ad, pso=pso: e.tensor_tensor(out=ad, in0=pso[:, 1, :, :], in1=ad, op=ALU.add),
                     reads=[t_pso[par], t_a_acc], writes=[t_a_acc])

        nb = len(ctxs)
        if own:
            for i in range(2):
                S.op("dve", lambda e, i=i: e.memset(a_QT[i], 0.0), writes=[t_a_QT[i]])
        st_P(ctxs[0]); st_EP(ctxs[0])
        for b in range(nb):
            if b + 1 < nb:
                st_P(ctxs[b + 1], which=("q", "k"))
            st_TR(ctxs[b])
            if b + 1 < nb:
                st_P(ctxs[b + 1], which=("v",))
            if own and b >= 1:
                st_PV(ctxs[b - 1])
            if b + 1 < nb:
                st_EP(ctxs[b + 1])
            if own and b >= 1:
                st_ACC(ctxs[b - 1])
            if own:
                st_QK(ctxs[b])
        if own:
            st_PV(ctxs[nb - 1])
            st_ACC(ctxs[nb - 1])
        if not own:
            return
        FIN_ALIAS = t_a_qk + t_a_qkr + t_a_QT + t_a_PT[0] + t_a_PT[1]
        S.op("dve", lambda e: e.memset(scratch[:], 0.0), writes=FIN_ALIAS + [t_a_fin, t_scr])
        for p in range(2):
            gp = 2 * half + p
            wz, twz = load_chunk(win_d[:, COL_ZA + gp * 128:COL_ZA + (gp + 1) * 128])
            for tt in range(4):
                pz = PB[tt % 2]
                tsl = slice(tt * 512, (tt + 1) * 512)
                for c in range(NCH):
                    S.op("pe", lambda e, pz=pz, wz=wz, c=c, tsl=tsl: e.matmul(pz[:, :], lhsT=wz[:, c, :], rhs=hT[:, c, tsl], start=(c == 0), stop=(c == NCH - 1)),
                         reads=[twz, t_hT], writes=[t_pqk[tt % 2]], inc=(c == NCH - 1))
                S.op("act", lambda e, pz=pz: e.activation(out=pz[:, :], in_=pz[:, :], func=AF.Silu), reads=[t_pqk[tt % 2]], writes=[t_pqk[tt % 2]])
                prd, tprd = PB[2 + tt % 2], t_PB[2 + tt % 2]
                S.op("dve", lambda e, p=p, tsl=tsl, prd=prd: e.reciprocal(out=prd[:, :], in_=a_accd[:, p, tsl]), reads=[t_a_acc], writes=[tprd])
                S.op("dve", lambda e, p=p, tsl=tsl, prd=prd: e.tensor_tensor(out=a_ft, in0=a_accn[:, p, tsl], in1=prd[:, :], op=ALU.mult),
                     reads=[t_a_acc, tprd], writes=[t_a_fin])
                S.op("dve", lambda e, gp=gp, tsl=tsl, pz=pz: e.tensor_tensor(out=agT[:, gp, tsl], in0=a_ft, in1=pz[:, :], op=ALU.mult),
                     reads=[t_a_fin, t_pqk[tt % 2]], writes=[t_agT[gp]])
        S.op("dve", lambda e: e.memset(scratch[:], 0.0), writes=FIN_ALIAS + [t_a_fin, t_scr])

    def glu_loads(c):
        return (load_chunk(win_d[:, COL_GA + c * 128:COL_GA + (c + 1) * 128]),
                load_chunk(win_d[:, COL_GB + c * 128:COL_GB + (c + 1) * 128]))

    def glu_chunk(c, tsl, ncol, uo, t_uo, flag, wts=None):
        (wa, twa), (wb_, twb_) = wts if wts is not None else glu_loads(c)
        pa, pb = PB[2 * (c % 2)], PB[2 * (c % 2) + 1]
        ta, tb = t_PB[2 * (c % 2)], t_PB[2 * (c % 2) + 1]
        for k in range(NCH):
            S.op("pe", lambda e, pa=pa, wa=wa, k=k: e.matmul(pa[:, 0:ncol], lhsT=wa[:, k, :], rhs=hT[:, k, tsl], start=(k == 0), stop=(k == NCH - 1)),
                 reads=[twa, t_hT], writes=[ta], inc=(k == NCH - 1))
        for k in range(NCH):
            S.op("pe", lambda e, pb=pb, wb_=wb_, k=k: e.matmul(pb[:, 0:ncol], lhsT=wb_[:, k, :], rhs=hT[:, k, tsl], start=(k == 0), stop=(k == NCH - 1)),
                 reads=[twb_, t_hT], writes=[tb], inc=(k == NCH - 1))
        th = c_th[c % 2]
        S.op("act", lambda e, th=th, pb=pb: e.activation(out=th[:, 0:ncol], in_=pb[:, 0:ncol], func=AF.Tanh, scale=0.5),
             reads=[tb], writes=[t_c_th[c % 2]])
        S.op("dve", lambda e, th=th, pa=pa, uo=uo: e.scalar_tensor_tensor(out=uo, in0=th[:, 0:ncol], scalar=1.0, in1=pa[:, 0:ncol],
                                                                          op0=ALU.add, op1=ALU.mult),
             reads=[t_c_th[c % 2], ta], writes=[t_uo])
        if flag:
            S.op("dve", lambda e, uo=uo: e.tensor_scalar(out=uo, in0=uo, scalar1=hflag[:, 0:1],
                                                         scalar2=None, op0=ALU.mult), reads=[t_uo, t_hflag], writes=[t_uo])

    def halo_u():
        switch_phase(CONV_T + t_PB)
        for c in range(NCH):
            glu_chunk(c, slice(SBT - 32, SBT), 32, uhist[:, c, :], t_uhist[c], True)

    def conv_tile(SB, tt):
        tsl = slice(tt * 512, (tt + 1) * 512)
        S.op("dve", lambda e: e.tensor_copy(out=c_uT[:, :, 0:32], in_=uhist[:, :, :]), reads=t_uhist, writes=t_c_uT + t_c_mgT)
        cw3 = cwh[:].rearrange("p (j c) -> p j c", c=8)

        gw = {}

        def diag_load(c):
            dg, tdg = c_diag[c % 2], t_c_diag[c % 2]
            S.dma("sp", lambda e, dg=dg, c=c: e.dma_start(out=dg, in_=dscr[c].rearrange("p (j n) -> p j n", j=31)), reads=[t_dscr], writes=[tdg])

        def st_G(c):
            if c + 1 < NCH:
                gw[c + 1] = glu_loads(c + 1)
            if not (c < 2 and tt > 0):
                diag_load(c)
            glu_chunk(c, tsl, 512, c_uT[:, c, 32:544], t_c_uT[c], False, wts=gw.pop(c))
            S.op("dve", lambda e, c=c: e.tensor_copy(out=uhist[:, c, :], in_=c_uT[:, c, 512:544]), reads=[t_c_uT[c]], writes=[t_uhist[c]])

        def st_C(c):
            pc, tpc = PB[4 + (c % 2)], t_PB[4 + (c % 2)]
            dg, tdg = c_diag[c % 2], t_c_diag[c % 2]
            for j in range(NTAP_PE):
                S.op("pe", lambda e, pc=pc, dg=dg, c=c, j=j: e.matmul(pc[:, :], lhsT=dg[:, j, :], rhs=c_uT[:, c, 2 + j:514 + j], start=(j == 0), stop=(j == NTAP_PE - 1)),
                     reads=[tdg, t_c_uT[c]], writes=[tpc], inc=(j == NTAP_PE - 1))
            for j in range(NTAP_PE, 31):
                S.op("dve", lambda e, pc=pc, c=c, j=j: e.scalar_tensor_tensor(out=pc[:, :], in0=c_uT[:, c, 2 + j:514 + j], scalar=cwh[:, j * 8 + c:j * 8 + c + 1],
                                                                              in1=pc[:, :], op0=ALU.mult, op1=ALU.add),
                     reads=[t_c_uT[c], t_cwh, tpc], writes=[tpc])
            cb = vT[:, 8 + c:9 + c]
            S.op("act", lambda e, pc=pc, c=c, cb=cb: e.activation(out=c_cv[:, c, :], in_=pc[:, :], func=AF.Identity, bias=cb), reads=[tpc, t_vT], writes=[t_c_cv[c], t_c_ucT[c]])
            S.op("act", lambda e, pc=pc, c=c, cb=cb: e.activation(out=c_cvb[c % 2], in_=pc[:, :], func=AF.Identity, bias=cb), reads=[tpc, t_vT], writes=[t_c_cvb[c % 2]])
            S.op("act", lambda e, pc=pc, c=c, cb=cb: e.activation(out=c_sqb[c % 2], in_=pc[:, :], func=AF.Square, bias=cb), reads=[tpc, t_vT], writes=[t_c_sqb[c % 2]])

        def st_S(c):
            S.op("pe", lambda e, c=c: e.matmul(PB[6][:, :], lhsT=onesdiv[:], rhs=c_cvb[c % 2], start=(c == 0), stop=(c == NCH - 1)),
                 reads=[t_ones, t_c_cvb[c % 2]], writes=[t_PB[6]], inc=True)
            S.op("pe", lambda e, c=c: e.matmul(PB[7][:, :], lhsT=onesdiv[:], rhs=c_sqb[c % 2], start=(c == 0), stop=(c == NCH - 1)),
                 reads=[t_ones, t_c_sqb[c % 2]], writes=[t_PB[7]], inc=True)

        gw[0] = glu_loads(0)
        st_G(0)
        for c in range(NCH):
            if c + 1 < NCH:
                st_G(c + 1)
            st_C(c)
            if c >= 1:
                st_S(c - 1)
        st_S(NCH - 1)
        if tt + 1 < 4:
            diag_load(0); diag_load(1)
        S.op("act", lambda e: e.activation(out=c_lt, in_=PB[6][:, :], func=AF.Square), reads=[t_PB[6]], writes=[t_c_lt] + t_c_ot)
        S.op("dve", lambda e: e.tensor_tensor(out=c_var, in0=PB[7][:, :], in1=c_lt, op=ALU.subtract), reads=[t_PB[7], t_c_lt], writes=[t_c_ln] + t_c_ot)
        S.op("act", lambda e: e.activation(out=c_var, in_=c_var, func=AF.Sqrt, bias=EPS), reads=[t_c_ln], writes=[t_c_ln])
        S.op("dve", lambda e: e.reciprocal(out=c_rstd, in_=c_var), reads=[t_c_ln], writes=[t_c_ln])
        S.op("dve", lambda e: e.scalar_tensor_tensor(out=PB[6][:, :], in0=PB[6][:, :], scalar=-1.0, in1=c_rstd, op0=ALU.mult, op1=ALU.mult),
             reads=[t_c_ln], writes=[t_PB[6]])
        S.op("dve", lambda e: e.tensor_copy(out=PB[7][:, :], in_=c_rstd), reads=[t_c_ln], writes=[t_PB[7]])
        szb = [c_szc, c_sl]; t_szb = [t_c_szc, t_c_sl]

        def st_Z(c):
            wz, twz = load_chunk(win_d[:, COL_ZC + c * 128:COL_ZC + (c + 1) * 128])
            pz, tpz = PB[c % 4], t_PB[c % 4]
            for k in range(NCH):
                S.op("pe", lambda e, pz=pz, wz=wz, k=k: e.matmul(pz[:, :], lhsT=wz[:, k, :], rhs=hT[:, k, tsl], start=(k == 0), stop=(k == NCH - 1)),
                     reads=[twz, t_hT], writes=[tpz], inc=(k == NCH - 1))
            S.op("act", lambda e, pz=pz, c=c: e.activation(out=szb[c % 2], in_=pz[:, :], func=AF.Silu), reads=[tpz], writes=[t_szb[c % 2]])

        st_Z(0)
        for c in range(NCH):
            if c + 1 < NCH:
                st_Z(c + 1)
            S.op("dve", lambda e, c=c: e.tensor_tensor(out=c_lt, in0=c_cv[:, c, :], in1=PB[7][:, :], op=ALU.mult), reads=[t_c_cv[c], t_PB[7]], writes=[t_c_lt])
            S.op("dve", lambda e: e.tensor_tensor(out=c_lt, in0=c_lt, in1=PB[6][:, :], op=ALU.add), reads=[t_c_lt, t_PB[6]], writes=[t_c_lt])
            psl, tpsl = PB[4 + (c % 2)], t_PB[4 + (c % 2)]
            S.op("act", lambda e, c=c, psl=psl: e.activation(out=psl[:, :], in_=c_lt, func=AF.Silu, scale=vT[:, 16 + c:17 + c], bias=vT[:, 24 + c:25 + c]),
                 reads=[t_c_lt, t_vT], writes=[tpsl])
            S.op("dve", lambda e, c=c, psl=psl: e.tensor_tensor(out=c_ucT[c], in0=szb[c % 2], in1=psl[:, :], op=ALU.mult),
                 reads=[t_szb[c % 2], tpsl], writes=[t_c_ucT[c], t_c_cv[c]])
        for n in range(NCH):
            wgc, twgc = load_chunk(win_d[:, COL_GC + n * 128:COL_GC + (n + 1) * 128])
            wga, twga = load_chunk(win_d[:, COL_GT + n * 128:COL_GT + (n + 1) * 128])
            wco, twco = load_chunk(wco_d[:, n * 128:(n + 1) * 128])
            wao, twao = load_chunk(wao_d[:, n * 128:(n + 1) * 128], kch=4)
            o = 4 * (n % 2)
            pgc, pga, pyc, pya = PB[o], PB[o + 1], PB[o + 2], PB[o + 3]
            tgc, tga, tyc, tya = t_PB[o], t_PB[o + 1], t_PB[o + 2], t_PB[o + 3]
            for k in range(NCH):
                S.op("pe", lambda e, pgc=pgc, wgc=wgc, k=k: e.matmul(pgc[:, :], lhsT=wgc[:, k, :], rhs=hT[:, k, tsl], start=(k == 0), stop=(k == NCH - 1)),
                     reads=[twgc, t_hT], writes=[tgc], inc=(k == NCH - 1))
            for k in range(NCH):
                S.op("pe", lambda e, pga=pga, wga=wga, k=k: e.matmul(pga[:, :], lhsT=wga[:, k, :], rhs=hT[:, k, tsl], start=(k == 0), stop=(k == NCH - 1)),
                     reads=[twga, t_hT], writes=[tga], inc=(k == NCH - 1))
            for k in range(NCH):
                S.op("pe", lambda e, pyc=pyc, wco=wco, k=k: e.matmul(pyc[:, :], lhsT=wco[:, k, :], rhs=c_ucT[k], start=(k == 0), stop=(k == NCH - 1)),
                     reads=[twco, t_c_ucT[k]], writes=[tyc], inc=(k == NCH - 1))
            for k in range(4):
                S.op("pe", lambda e, pya=pya, wao=wao, k=k: e.matmul(pya[:, :], lhsT=wao[:, k, :], rhs=agT[:, k, tsl], start=(k == 0), stop=(k == 3)),
                     reads=[twao, t_agT[k]], writes=[tya], inc=(k == 3))
            thc, tha = c_th[0], c_th[1]
            S.op("act", lambda e, pgc=pgc: e.activation(out=thc, in_=pgc[:, :], func=AF.Tanh, scale=0.5), reads=[tgc], writes=[t_c_th[0]])
            S.op("act", lambda e, pga=pga: e.activation(out=tha, in_=pga[:, :], func=AF.Tanh, scale=0.5), reads=[tga], writes=[t_c_th[1]])
            S.op("dve", lambda e, pyc=pyc: e.scalar_tensor_tensor(out=pyc[:, :], in0=thc, scalar=1.0, in1=pyc[:, :], op0=ALU.add, op1=ALU.mult),
                 reads=[t_c_th[0], tyc], writes=[tyc])
            S.op("dve", lambda e, pya=pya: e.scalar_tensor_tensor(out=c_m2, in0=tha, scalar=1.0, in1=pya[:, :], op0=ALU.add, op1=ALU.mult),
                 reads=[t_c_th[1], tya], writes=[t_c_m2])
            S.op("dve", lambda e, pyc=pyc, n=n: e.tensor_tensor(out=c_mgT[n], in0=c_m2, in1=pyc[:, :], op=ALU.add),
                 reads=[t_c_m2, tyc], writes=[t_c_mgT[n]] + t_c_uT)
        def x_loads(SB_, tt_, s0_):
            key = (SB_, tt_, s0_)
            if key in xpref:
                return
            xpref.add(key)
            for s in range(2):
                row_in = SB_ * SBT + tt_ * 512 + (s0_ + s) * 128
                S.dma("sp", lambda e, s=s, row_in=row_in: e.dma_start(out=xn2[:, s, :], in_=x_d[row_in:row_in + 128, :]), writes=[t_xn[s]])

        for s0 in range(0, 4, 2):
            x_loads(SB, tt, s0)
            for s in range(2):
                sub = s0 + s
                o = 4 * s
                for hh in range(2):
                    py, tpy = PB[o + hh], t_PB[o + hh]
                    for k in range(NCH):
                        S.op("pe", lambda e, py=py, k=k, sub=sub, hh=hh: e.matmul(py[:, :], lhsT=c_mgT[k][:, sub * 128:(sub + 1) * 128],
                                                                                rhs=wog[:, k, hh * 512:(hh + 1) * 512], start=(k == 0), stop=(k == NCH - 1)),
                             reads=[t_c_mgT[k], t_wog], writes=[tpy], inc=(k == NCH - 1))
                    S.op("dve", lambda e, py=py, s=s, hh=hh: e.tensor_tensor(out=c_ot[:, s, hh * 512:(hh + 1) * 512], in0=py[:, :],
                                                                             in1=xn2[:, s, hh * 512:(hh + 1) * 512], op=ALU.add),
                         reads=[tpy, t_xn[s]], writes=[t_c_ot[s], t_c_ln, t_c_lt])
                S.op("act", lambda e, s=s: e.activation(out=c_sl.bitcast(BF16)[:, 0:1024], in_=c_ot[:, s, :], func=AF.Square, accum_out=fsc[:, s:s + 1]),
                     reads=[t_c_ot[s]], writes=[t_c_sl, t_fsc])
            if s0 == 0:
                x_loads(SB, tt, 2)
            elif tt + 1 < 4:
                x_loads(SB, tt + 1, 0)
            S.op("act", lambda e: e.activation(out=fsc[:, 2:4], in_=fsc[:, 0:2], func=AF.Sqrt, scale=1.0 / 1024.0, bias=EPS), reads=[t_fsc], writes=[t_fsc])
            S.op("dve", lambda e: e.reciprocal(out=fsc[:, 4:6], in_=fsc[:, 2:4]), reads=[t_fsc], writes=[t_fsc])
            for s in range(2):
                sub = s0 + s
                row_out = (SB - 1) * SBT + tt * 512 + sub * 128
                S.op("dve", lambda e, s=s: e.scalar_tensor_tensor(out=c_ot[:, s, :], in0=c_ot[:, s, :], scalar=fsc[:, 4 + s:5 + s], in1=fgb[:],
                                                                  op0=ALU.mult, op1=ALU.mult), reads=[t_fsc, t_fgb], writes=[t_c_ot[s]])
                S.dma("sp", lambda e, s=s, row_out=row_out: e.dma_start(out=out_d[row_out:row_out + 128, :], in_=c_ot[:, s, :]), reads=[t_c_ot[s]])

    xpref = set()
    import os
    kstop = int(os.environ.get("KSTOP", "999"))
    step = [0]

    def stop():
        step[0] += 1
        print("STEP", step[0], "ops so far", S.total)
        return step[0] > kstop

    def main():
        if stop(): return
        for SB in range(NSB):
            phase_x(SB)
            if stop(): return
            switch_phase(ATT_T + ATT_P + t_PB)
            attention(SB, 0)
            if stop(): return
            attention(SB, 1)
            if stop(): return
            if SB == 0:
                halo_u()
                if stop(): return
            else:
                switch_phase(CONV_T + t_PB)
                for tt in range(4):
                    conv_tile(SB, tt)
                    if stop(): return
    main()
    S.wait_all("sp", t_xn + t_c_ot + state["phase_T"] + [t_hT, t_wog, t_fgb, t_mask, t_hflag, t_cwh, t_mod, t_tab])
    S.emit()
    return nc


def make_in_maps(x, c, positions, norm_g, w_ada, b_ada, w_in, conv_w, conv_b, conv_ln_g, conv_ln_b,
                 w_conv_out, w_att_out, w_o, final_g):
    f32 = np.float32
    ki = np.arange(128)[:, None]
    qi = np.arange(128)[None, :]
    prev = np.where(ki >= qi, 0.0, NEG).astype(f32)
    cur = np.where(ki <= qi, 0.0, NEG).astype(f32)
    allneg = np.full((128, 128), NEG, f32)
    shared = dict(
        cw=np.ascontiguousarray(np.asarray(conv_w[0], f32).reshape(248, 128)),
        b_ada=np.ascontiguousarray(np.asarray(b_ada[0], f32).reshape(1, 3 * D)),
        w_ada=np.ascontiguousarray(np.asarray(w_ada[0], f32)),
        w_in=np.ascontiguousarray(np.asarray(w_in[0], f32)),
        w_conv_out=np.ascontiguousarray(np.asarray(w_conv_out[0], f32)),
        w_att_out=np.ascontiguousarray(np.asarray(w_att_out[0], f32)),
        w_o=np.ascontiguousarray(np.asarray(w_o[0], f32)),
        final_g=np.ascontiguousarray(np.asarray(final_g, f32).reshape(1, D)),
    )
    maps = []
    for core in range(8):
        b, hf = divmod(core, 2)
        xl = np.zeros((TOKL, D), f32)
        pl = np.zeros((TOKL,), np.int32)
        if hf == 0:
            xl[SBT:] = x[b, 0:OWN]
            pl[SBT:] = positions[b, 0:OWN]
        else:
            xl[:] = x[b, OWN - SBT:]
            pl[:] = positions[b, OWN - SBT:]
        vecs = np.concatenate([np.asarray(norm_g[0], f32).reshape(8, 128), np.asarray(conv_b[0], f32).reshape(8, 128),
                               np.asarray(conv_ln_g[0], f32).reshape(8, 128), np.asarray(conv_ln_b[0], f32).reshape(8, 128),
                               np.asarray(c[b], f32).reshape(8, 128)], 0)
        masks = np.concatenate([prev, cur, allneg if hf == 0 else prev, cur], 1)
        m = dict(shared)
        m.update(x=xl, pos=pl, vecs=np.ascontiguousarray(vecs), masks=np.ascontiguousarray(masks),
                 hflag=np.full((128, 1), float(hf), f32))
        maps.append(m)
    return maps


_NC_CACHE = {}


def kernel(x, c, positions, norm_g, w_ada, b_ada, w_in, conv_w, conv_b, conv_ln_g, conv_ln_b,
           w_conv_out, w_att_out, w_o, final_g):
    x = np.asarray(x); positions = np.asarray(positions)
    maps = make_in_maps(x, np.asarray(c), positions, np.asarray(norm_g), np.asarray(w_ada), np.asarray(b_ada), np.asarray(w_in),
                        np.asarray(conv_w), np.asarray(conv_b), np.asarray(conv_ln_g), np.asarray(conv_ln_b),
                        np.asarray(w_conv_out), np.asarray(w_att_out), np.asarray(w_o), np.asarray(final_g))
    nc = build_program()
    res = run_bass_kernel_spmd(nc, maps, core_ids=list(range(8)))
    out = np.empty((4, 8192, D), np.float32)
    for core in range(8):
        b, hf = divmod(core, 2)
        out[b, hf * OWN:(hf + 1) * OWN] = res.results[core]["out"]
    return out
```

```python
import contextlib
import numpy as np
import ml_dtypes
import concourse.bass as bass
import concourse.mybir as mybir
from concourse.bass_utils import run_bass_kernel_spmd

F32 = mybir.dt.float32
BF16 = mybir.dt.bfloat16
I32 = mybir.dt.int32
AF = mybir.ActivationFunctionType
ALU = mybir.AluOpType

ENGS = ("pe", "act", "dve", "pool", "sp")


class T:
    __slots__ = ("name", "w", "r", "dsem", "dcnt", "excl")

    def __init__(self, name, excl=False):
        self.name = name
        self.excl = excl
        self.w = None
        self.r = []
        self.dsem = None
        self.dcnt = 0


class Sched:
    def __init__(self, nc):
        self.nc = nc
        self.ops = {e: [] for e in ENGS}
        self.dma_tiles = []
        import os
        self.total = 0
        self.limit = int(os.environ.get("OPLIMIT", "100000000"))

    @staticmethod
    def _deps(reads, writes):
        deps = []
        for t in reads:
            if t.w is not None:
                deps.append(t.w)
        for t in writes:
            if t.w is not None:
                deps.append(t.w)
            deps.extend(t.r)
        return deps

    def op(self, eng, fn, reads=(), writes=(), inc=True):
        self.total += 1
        if self.total > self.limit:
            return None
        writes = list(writes) + [t for t in reads if t.excl]
        reads = [t for t in reads if not t.excl]
        deps = self._deps(reads, writes)
        seq = len(self.ops[eng])
        self.ops[eng].append(dict(fn=fn, deps=deps, inc=inc, dma=None))
        me = ("e", eng, seq)
        for t in reads:
            t.r.append(me)
        for t in writes:
            t.w = me
            t.r = []
        return me

    def dma(self, eng, fn, reads=(), writes=(), key=None):
        self.total += 1
        if self.total > self.limit:
            return None
        if key is None:
            key = writes[0] if writes else reads[0]
        deps = self._deps(reads, writes)
        if key.dsem is None:
            key.dsem = True
            self.dma_tiles.append(key)
        key.dcnt += 16
        me = ("d", key, key.dcnt)
        self.ops[eng].append(dict(fn=fn, deps=deps, inc=False, dma=key))
        for t in reads:
            t.r.append(me)
        for t in writes:
            t.w = me
            t.r = []
        return me

    def wait_all(self, eng, tiles):
        deps = []
        for t in tiles:
            if t.w is not None:
                deps.append(t.w)
            deps.extend(t.r)
        self.ops[eng].append(dict(fn=None, deps=deps, inc=False, dma=None))

    def emit(self):
        nc = self.nc
        with contextlib.ExitStack() as st:
            esem = {e: st.enter_context(nc.semaphore("s_" + e)) for e in ENGS}
            for i, t in enumerate(self.dma_tiles):
                t.dsem = st.enter_context(nc.semaphore("d%d" % i))
            need = {}
            for e in ENGS:
                ops = self.ops[e]
                n = len(ops)
                c = 0
                incs = []
                for o in ops:
                    if o["inc"]:
                        c += 1
                    incs.append(c)
                nd = [None] * n
                nxt = None
                for i in range(n - 1, -1, -1):
                    if ops[i]["inc"]:
                        nxt = incs[i]
                    nd[i] = nxt
                need[e] = nd
            block = st.enter_context(nc.Block())

            def make(e):
                def body(engh):
                    waited = {}
                    for i, o in enumerate(self.ops[e]):
                        req = {}
                        for d in o["deps"]:
                            if d[0] == "e":
                                _, se, sq = d
                                if se == e and e == "pe":
                                    continue
                                v = need[se][sq]
                                if v is None and self.limit < 100000000:
                                    continue
                                assert v is not None, ("dep on non-inc tail op", se, sq)
                                if se == e:
                                    assert sq < i
                                k = ("e", se)
                                sem = esem[se]
                            else:
                                _, kt, v = d
                                k = ("d", id(kt))
                                sem = kt.dsem
                            if waited.get(k, 0) >= v:
                                continue
                            if k not in req or req[k][1] < v:
                                req[k] = (sem, v)
                        for k, (sem, v) in req.items():
                            engh.wait_ge(sem, v)
                            waited[k] = v
                        if o["fn"] is None:
                            continue
                        ins = o["fn"](engh)
                        if o["dma"] is not None:
                            ins.then_inc(o["dma"].dsem, 16)
                        elif o["inc"]:
                            ins.then_inc(esem[e], 1)
                return body

            block.tensor(make("pe"))
            block.scalar(make("act"))
            block.vector(make("dve"))
            block.gpsimd(make("pool"))
            block.sync(make("sp"))


D = 1024
NCH = 8
SBT = 2048
NSB = 3
TOKL = NSB * SBT
OWN = 4096
DIL = (1, 4, 16)
RSL = (3, 6, 18)
EPS = 1e-6
COL_GA, COL_GB, COL_ZC, COL_Q, COL_K, COL_V, COL_ZA, COL_GC, COL_GT = 0, 1024, 2048, 3072, 4608, 6144, 7680, 8192, 9216
IN_COLS = 10240
NEG = -30000.0
ARENA_BYTES = 62464
NWB = 4
NTAP_PE = 26
HALO_BLOCKS = {0: [15], 1: [12, 13, 14, 15], 2: list(range(16))}
INV_FREQ = (np.float32(500000.0) ** (-(np.arange(8, dtype=np.float32) * np.float32(2.0) / np.float32(16.0)))).astype(np.float32)


def build_program(debug=False):
    nc = bass.Bass("TRN2", target_bir_lowering=False)
    dt_in = lambda name, shape, dt=F32: nc.dram_tensor(name, list(shape), dt, kind="ExternalInput").ap()
    x_d = dt_in("x", [TOKL, D])
    pos_d = dt_in("pos", [TOKL], I32)
    vecs_d = dt_in("vecs", [40, 128])
    cw_d = dt_in("cw", [248, 128])
    bada_d = dt_in("b_ada", [1, 3 * D])
    wada_d = dt_in("w_ada", [D, 3 * D])
    win_d = dt_in("w_in", [D, IN_COLS])
    wco_d = dt_in("w_conv_out", [D, D])
    wao_d = dt_in("w_att_out", [512, D])
    wo_d = dt_in("w_o", [D, D])
    fg_d = dt_in("final_g", [1, D])
    mask_d = dt_in("masks", [128, 512])
    hflag_d = dt_in("hflag", [128, 1])
    out_d = nc.dram_tensor("out", [OWN, D], F32, kind="ExternalOutput").ap()
    dscr = nc.dram_tensor("diag_scr", [8, 128, 31 * 128], BF16).ap()
    t_dscr = T("dscr")

    S = Sched(nc)
    A = nc.alloc_sbuf_tensor

    hT = A("hT", [128, NCH, SBT], BF16); t_hT = T("hT")
    KR = {}; VR = {}; t_KR = {}; t_VR = {}
    for g in range(3):
        for hf in range(2):
            KR[g, hf] = A("KR%d%d" % (g, hf), [128, RSL[g], 2, 128], BF16)
            VR[g, hf] = A("VR%d%d" % (g, hf), [128, RSL[g], 4, 64], BF16)
            t_KR[g, hf] = [T("KR%d%d_%d" % (g, hf, s)) for s in range(RSL[g])]
            t_VR[g, hf] = [T("VR%d%d_%d" % (g, hf, s)) for s in range(RSL[g])]
    agT = A("agT", [128, 4, SBT], BF16); t_agT = [T("agT%d" % p) for p in range(4)]
    wog = A("wog", [128, NCH, D], BF16); t_wog = T("wog")
    fgb = A("fgb", [128, D], F32); t_fgb = T("fgb")
    tcos = A("tcos", [128, 48, 8], F32); tsin = A("tsin", [128, 48, 8], F32); t_tab = T("tab")
    ident = A("ident", [128, 128], BF16); identf = A("identf", [128, 128], F32); t_id = T("ident")
    ones64 = A("ones64", [128, 64], BF16); onesdiv = A("onesdiv", [128, 128], BF16)
    onesf = A("onesf", [1, 128], F32); t_ones = T("ones")
    maskb = A("maskb", [128, 512], BF16); t_mask = T("mask")
    hflag = A("hflag_s", [128, 1], F32); t_hflag = T("hflag")
    vT = A("vT", [128, 40], F32); t_vT = T("vT")
    cwh = A("cwh", [128, 248], F32); t_cwh = T("cwh")
    modT = A("modT", [128, 16], F32); gmod = A("gmod", [128, 8], F32); t_mod = T("mod")
    ss = A("ss", [128, 2], F32); sd = A("sd", [128, 2], F32); rs2 = A("rs2", [128, 2], F32)
    t_ss = T("ss")
    fsc = A("fsc", [128, 8], F32); t_fsc = T("fsc")
    wbuf = [A("wbuf%d" % i, [128, NCH, 128], BF16) for i in range(NWB)]
    t_wbuf = [T("wbuf%d" % i) for i in range(NWB)]
    xn2 = A("xn2", [128, 2, D], F32); t_xn = [T("xn0"), T("xn1")]
    uhist = A("uhist", [128, NCH, 32], BF16); t_uhist = [T("uhist%d" % c) for c in range(NCH)]
    arena = A("arena", [128, ARENA_BYTES // 4], F32)

    def carve(off, nbytes, dt, pattern=None, **kw):
        assert off % 4 == 0 and nbytes % 4 == 0 and off + nbytes <= ARENA_BYTES
        v = arena[:, off // 4:(off + nbytes) // 4]
        if dt != F32:
            v = v.bitcast(dt)
        if pattern:
            v = v.rearrange(pattern, **kw)
        return v

    su_wada = [carve(i * 16384, 16384, F32, "p (c n) -> p c n", c=8) for i in range(2)]
    t_su_wada = [T("su_wada0"), T("su_wada1")]
    su_modrow = carve(32768, 12288, F32)[0:1, :]; t_su_modrow = T("su_modrow")
    su_gate = carve(45056, 4096, F32); t_su_gate = T("su_gate")
    su_wo = carve(49152, 8192, F32, "p (c n) -> p c n", c=8); t_su_wo = T("su_wo")
    su_vecs = carve(57344, 512, F32)[0:40, :]; t_su_vecs = T("su_vecs")
    su_cw = [carve(57856 + i * 512, 512, F32)[0:124, :] for i in range(2)]; t_su_cw = T("su_cw")
    su_bada = carve(58880, 4096 * 3, F32)[0:1, :] if False else None
    su_maskf = carve(58880, 2048, F32); t_su_maskf = T("su_maskf")
    su_one = carve(60928, 4, F32)[0:1, :]; t_su_one = T("su_one")
    su_diag = [hT[:, 2 * i:2 * i + 2, :].rearrange("p a t -> p (a t)")[:, 0:3968].rearrange("p (j n) -> p j n", j=31) for i in range(2)]
    t_su_diag = [T("su_diag0"), T("su_diag1")]
    SETUP_T = t_su_diag + [t_hT, t_su_modrow, t_su_gate, t_su_wo, t_su_vecs, t_su_cw, t_su_maskf, t_su_one] + t_su_wada
    x_xb = [carve(i * 2048, 2048, BF16) for i in range(2)]; t_x_xb = [T("x_xb0"), T("x_xb1")]
    x_sq = carve(4096, 2048, BF16); t_x_sq = T("x_sq")
    x_posi = carve(6144, 48 * 4, I32); x_posf = carve(6400, 48 * 4, F32)
    x_ang = carve(8192, 1536, F32, "p (b j) -> p b j", j=8)
    x_y = [carve(9728 + i * 1536, 1536, F32) for i in range(2)]
    x_ki = carve(12800, 1536, I32); x_kf = carve(14336, 1536, F32); x_m = carve(15872, 1536, F32)
    t_x_tab = T("x_tabtmp")
    X_T = t_x_xb + [t_x_sq, t_x_tab]
    a_accn = carve(0, 16384, F32, "p (a t) -> p a t", a=2); a_accd = carve(16384, 16384, F32, "p (a t) -> p a t", a=2)
    t_a_acc = T("a_acc")
    a_wq = [carve(32768 + i * 4096, 4096, BF16, "p (c n) -> p c n", c=8) for i in range(5)]
    t_a_wq = [T("a_wq%d" % i) for i in range(5)]
    a_qk = [carve(53248 + i * 1024, 1024, BF16) for i in range(2)]
    t_a_qk = [T("a_qk0"), T("a_qk1")]; t_a_qkr = [T("a_qkr0"), T("a_qkr1")]
    a_QT = [carve(55296 + i * 1024, 1024, BF16, "p (a h t) -> p a h t", a=2, h=2) for i in range(2)]; t_a_QT = [T("a_QT0"), T("a_QT1")]
    a_PT = [carve(57344 + i * 2048, 2048, BF16, "p (h t) -> p h t", h=4) for i in range(2)]
    t_a_PT = [[T("a_PT%d%d" % (i, p)) for p in range(2)] for i in range(2)]
    a_rt = carve(61440, 1024, F32, "p (k h j) -> p k h j", k=4, h=8); t_a_rt = [T("a_rt%d" % k) for k in range(4)]
    a_rd = carve(53248, 2048, F32); a_sz = carve(55296, 2048, F32); a_ft = carve(57344, 2048, F32)
    t_a_fin = T("a_fin")
    ATT_T = [t_a_acc, t_a_fin] + t_a_rt + t_a_wq + t_a_qk + t_a_qkr + t_a_QT + t_a_PT[0] + t_a_PT[1]
    c_cv = carve(0, 16384, F32, "p (c t) -> p c t", c=8); t_c_cv = [T("c_cv%d" % c) for c in range(8)]
    UW = 544
    c_uT = carve(16384, 8 * UW * 2, BF16, "p (c t) -> p c t", c=8); t_c_uT = [T("c_uT%d" % c) for c in range(8)]
    c_ucT = [carve(c * 2048, 1024, BF16) for c in range(8)]; t_c_ucT = [T("c_ucT%d" % c) for c in range(8)]
    c_mgT = [carve(16384 + n * 1024, 1024, BF16) for n in range(8)]; t_c_mgT = [T("c_mgT%d" % c) for c in range(8)]
    c_diag = [carve(25088 + i * 8192, 7936, BF16, "p (j n) -> p j n", j=31) for i in range(2)]; t_c_diag = [T("c_diag0"), T("c_diag1")]
    c_cvb = [carve(41472 + i * 1024, 1024, BF16) for i in range(2)]; t_c_cvb = [T("c_cvb0"), T("c_cvb1")]
    c_sqb = [carve(43520 + i * 1024, 1024, BF16) for i in range(2)]; t_c_sqb = [T("c_sqb0"), T("c_sqb1")]
    c_th = [carve(45568 + i * 2048, 2048, F32) for i in range(2)]; t_c_th = [T("c_th0"), T("c_th1")]
    c_var = carve(49664, 2048, F32); c_rstd = carve(51712, 2048, F32); c_nmr = carve(53760, 2048, F32)
    t_c_ln = T("c_ln")
    c_lt = carve(55808, 2048, F32); t_c_lt = T("c_lt")
    c_szc = carve(57856, 2048, F32); t_c_szc = T("c_szc")
    c_sl = carve(59904, 2048, F32); t_c_sl = T("c_sl")
    c_m2 = c_lt; t_c_m2 = t_c_lt
    c_ot = carve(49664, 8192, F32, "p (s n) -> p s n", s=2); t_c_ot = [T("c_ot0"), T("c_ot1")]
    CONV_T = (t_c_ot + t_c_cv + t_c_uT + t_c_ucT + t_c_mgT + t_c_cvb + t_c_sqb + t_c_th + t_c_diag
              + [t_c_ln, t_c_lt, t_c_szc, t_c_sl])

    PB = [nc.alloc_psum_tensor("pb%d" % i, [128, 512], F32) for i in range(8)]
    t_PB = [T("pb%d" % i, excl=True) for i in range(8)]
    PBh = [PB[i][:].bitcast(BF16) for i in range(8)]
    t_pqk = [t_PB[0], t_PB[1]]; t_pv = [t_PB[2], t_PB[2]]; t_ptr = [t_PB[3], t_PB[3]]
    t_pss = [t_PB[4], t_PB[5]]; t_pso = [t_PB[6], t_PB[7]]
    ATT_P = []
    scratch = A("fence_scratch", [128, 1], F32); t_scr = T("scr")

    state = {"phase_T": SETUP_T + t_PB}

    def switch_phase(new_T):
        old = state["phase_T"]
        S.op("dve", lambda e: e.memset(scratch[:], 0.0), writes=list(dict.fromkeys(list(old) + list(new_T) + [t_scr])))
        state["phase_T"] = list(new_T)

    wstate = {"i": 0}

    def load_chunk(src_ap, kch=NCH):
        i = wstate["i"] % NWB
        wstate["i"] += 1
        b, tb = wbuf[i], t_wbuf[i]
        S.dma("pool", lambda e, b=b, s=src_ap, k=kch: e.dma_start(out=b[:, 0:k, :], in_=s.rearrange("(c p) n -> p c n", p=128)),
              writes=[tb])
        return b, tb

    import os as _os
    sstop = int(_os.environ.get('SSTOP', '99'))
    def setup():
        S.dma("sp", lambda e: e.dma_start(out=su_vecs, in_=vecs_d[:, :]), writes=[t_su_vecs])
        for i in range(2):
            S.dma("sp", lambda e, i=i: e.dma_start(out=su_cw[i], in_=cw_d[i * 124:(i + 1) * 124, :]), writes=[t_su_cw])
        S.dma("sp", lambda e: e.dma_start(out=su_modrow, in_=bada_d[:, :]), writes=[t_su_modrow])
        S.dma("sp", lambda e: e.dma_start(out=su_maskf, in_=mask_d[:, :]), writes=[t_su_maskf])
        S.dma("sp", lambda e: e.dma_start(out=hflag[:], in_=hflag_d[:, :]), writes=[t_hflag])
        S.dma("sp", lambda e: e.dma_start(out=fgb[:], in_=fg_d.partition_broadcast(128)), writes=[t_fgb])
        if sstop < 2: return
        S.op("dve", lambda e: e.memset(identf[:], 0.0), writes=[t_id])
        S.op("pool", lambda e: e.affine_select(out=identf[:], in_=identf[:], pattern=[[-1, 128]], compare_op=ALU.not_equal,
                                               fill=1.0, base=0, channel_multiplier=1), reads=[t_id], writes=[t_id])
        S.op("dve", lambda e: e.tensor_copy(out=ident[:], in_=identf[:]), reads=[t_id], writes=[t_id])
        S.op("dve", lambda e: e.memset(ones64[:], 1.0), writes=[t_ones])
        S.op("dve", lambda e: e.memset(onesdiv[:], 1.0 / 1024.0), writes=[t_ones])
        S.op("dve", lambda e: e.memset(onesf[:], 1.0), writes=[t_ones])
        S.op("dve", lambda e: e.memset(su_one, 1.0), writes=[t_su_one])
        S.op("dve", lambda e: e.tensor_copy(out=maskb[:], in_=su_maskf), reads=[t_su_maskf], writes=[t_mask])
        if sstop < 3: return
        S.op("pe", lambda e: e.transpose(PB[0][:, 0:40], su_vecs, identf[0:40, 0:40]), reads=[t_su_vecs, t_id], writes=[t_PB[0]])
        S.op("dve", lambda e: e.tensor_copy(out=vT[:], in_=PB[0][:, 0:40]), reads=[t_PB[0]], writes=[t_vT])
        for i in range(2):
            S.op("pe", lambda e, i=i: e.transpose(PB[1][:, i * 124:(i + 1) * 124], su_cw[i], identf[0:124, 0:124]),
                 reads=[t_su_cw, t_id], writes=[t_PB[1]])
        S.op("dve", lambda e: e.tensor_scalar(out=cwh[:], in0=PB[1][:, 0:248], scalar1=0.5, scalar2=None, op0=ALU.mult),
             reads=[t_PB[1]], writes=[t_cwh])
        cw3s = cwh[:].rearrange("p (j c) -> p j c", c=8)
        for c in range(NCH):
            sdg, tsd = su_diag[c % 2], t_su_diag[c % 2]
            S.op("dve", lambda e, sdg=sdg, c=c: e.tensor_tensor(out=sdg, in0=ident[:].unsqueeze(1).broadcast_to([128, 31, 128]),
                                                               in1=cw3s[:, :, c].unsqueeze(2).broadcast_to([128, 31, 128]), op=ALU.mult),
                 reads=[t_id, t_cwh], writes=[tsd])
            S.dma("sp", lambda e, sdg=sdg, c=c: e.dma_start(out=dscr[c].rearrange("p (j n) -> p j n", j=31), in_=sdg), reads=[tsd], writes=[t_dscr])
        if sstop < 4: return
        for cb in range(6):
            wb_, twb_ = su_wada[cb % 2], t_su_wada[cb % 2]
            S.dma("sp", lambda e, wb_=wb_, cb=cb: e.dma_start(
                out=wb_, in_=wada_d[:, cb * 512:(cb + 1) * 512].rearrange("(c p) n -> p c n", p=128)), writes=[twb_])
            pbank = 2 + (cb % 2)
            for k in range(NCH):
                S.op("pe", lambda e, wb_=wb_, k=k, pbank=pbank: e.matmul(PB[pbank][0:1, :], lhsT=vT[:, 32 + k:33 + k], rhs=wb_[:, k, :],
                                                                        start=(k == 0), stop=(k == NCH - 1)),
                     reads=[twb_, t_vT], writes=[t_PB[pbank]], inc=(k == NCH - 1))
            S.op("dve", lambda e, cb=cb, pbank=pbank: e.tensor_tensor(out=su_modrow[:, cb * 512:(cb + 1) * 512], in0=PB[pbank][0:1, :],
                                                                      in1=su_modrow[:, cb * 512:(cb + 1) * 512], op=ALU.add),
                 reads=[t_PB[pbank], t_su_modrow], writes=[t_su_modrow])
        if sstop < 5: return
        for j in range(16):
            S.op("pe", lambda e, j=j: e.matmul(PB[4][:, j:j + 1], lhsT=su_modrow[:, j * 128:(j + 1) * 128], rhs=su_one,
                                               start=True, stop=True),
                 reads=[t_su_modrow, t_su_one], writes=[t_PB[4]], inc=(j == 15))
        S.op("dve", lambda e: e.tensor_copy(out=modT[:], in_=PB[4][:, 0:16]), reads=[t_PB[4]], writes=[t_mod])
        S.op("dve", lambda e: e.scalar_tensor_tensor(out=gmod[:], in0=modT[:, 8:16], scalar=1.0, in1=vT[:, 0:8],
                                                     op0=ALU.add, op1=ALU.mult), reads=[t_mod, t_vT], writes=[t_mod])
        for hh in range(2):
            S.op("pe", lambda e, hh=hh: e.matmul(PB[5 + hh][:, :], lhsT=onesf[:, :], rhs=su_modrow[:, 2048 + hh * 512:2048 + (hh + 1) * 512],
                                                 start=True, stop=True), reads=[t_ones, t_su_modrow], writes=[t_PB[5 + hh]])
            S.op("act", lambda e, hh=hh: e.activation(out=su_gate[:, hh * 512:(hh + 1) * 512], in_=PB[5 + hh][:, :], func=AF.Copy, scale=0.5),
                 reads=[t_PB[5 + hh]], writes=[t_su_gate])
        if sstop < 6: return
        for cb in range(4):
            S.dma("sp", lambda e, cb=cb: e.dma_start(out=su_wo, in_=wo_d[:, cb * 256:(cb + 1) * 256].rearrange("(c p) n -> p c n", p=128)),
                  writes=[t_su_wo])
            S.op("dve", lambda e, cb=cb: e.tensor_tensor(out=wog[:, :, cb * 256:(cb + 1) * 256], in0=su_wo,
                                                         in1=su_gate[:, cb * 256:(cb + 1) * 256].unsqueeze(1).broadcast_to([128, NCH, 256]),
                                                         op=ALU.mult), reads=[t_su_wo, t_su_gate], writes=[t_wog])

    setup()
    def phase_x(SB):
        switch_phase(X_T + t_PB)
        tabq = []

        def TABQ(fn, *a_, **k_):
            tabq.append((fn, a_, k_))

        def tab_emit(n):
            for _ in range(n):
                if tabq:
                    fn, a_, k_ = tabq.pop(0)
                    fn(*a_, **k_)

        for g in range(3):
            d = DIL[g]
            src = pos_d[SB * SBT:(SB + 1) * SBT].rearrange("(n i r) -> i n r", i=128, r=d)
            dst = x_posi[:, g * 16:(g + 1) * 16].rearrange("p (n r) -> p n r", r=d)
            TABQ(S.dma, "sp", lambda e, src=src, dst=dst: e.dma_start(out=dst, in_=src, allow_slow_non_contiguous=True), writes=[t_x_tab])
        TABQ(S.op, "dve", lambda e: e.tensor_copy(out=x_posf, in_=x_posi), reads=[t_x_tab], writes=[t_x_tab])
        for j in range(8):
            TABQ(S.op, "dve", lambda e, j=j: e.tensor_scalar(out=x_ang[:, :, j], in0=x_posf, scalar1=float(INV_FREQ[j]), scalar2=None,
                                                       op0=ALU.mult), reads=[t_x_tab], writes=[t_x_tab])
        angf = x_ang.rearrange("p b j -> p (b j)")
        for which in range(2):
            y = x_y[which]
            TABQ(S.op, "dve", lambda e, y=y, which=which: e.tensor_scalar(out=y, in0=angf, scalar1=float(1.0 / (2.0 * np.pi)),
                                                                    scalar2=0.25 * which, op0=ALU.mult, op1=ALU.add),
                 reads=[t_x_tab], writes=[t_x_tab])
            TABQ(S.op, "dve", lambda e, y=y: e.tensor_copy(out=x_ki, in_=y), reads=[t_x_tab], writes=[t_x_tab])
            TABQ(S.op, "dve", lambda e: e.tensor_copy(out=x_kf, in_=x_ki), reads=[t_x_tab], writes=[t_x_tab])
            TABQ(S.op, "dve", lambda e, y=y: e.tensor_tensor(out=y, in0=y, in1=x_kf, op=ALU.subtract), reads=[t_x_tab], writes=[t_x_tab])
            TABQ(S.op, "dve", lambda e, y=y: e.tensor_scalar(out=x_m, in0=y, scalar1=0.5, scalar2=None, op0=ALU.is_gt),
                 reads=[t_x_tab], writes=[t_x_tab])
            TABQ(S.op, "dve", lambda e, y=y: e.tensor_tensor(out=y, in0=y, in1=x_m, op=ALU.subtract), reads=[t_x_tab], writes=[t_x_tab])
            TABQ(S.op, "dve", lambda e, y=y: e.tensor_scalar(out=x_m, in0=y, scalar1=-0.5, scalar2=None, op0=ALU.is_lt),
                 reads=[t_x_tab], writes=[t_x_tab])
            TABQ(S.op, "dve", lambda e, y=y: e.tensor_tensor(out=y, in0=y, in1=x_m, op=ALU.add), reads=[t_x_tab], writes=[t_x_tab])
            tab = (tsin, tcos)[which]
            TABQ(S.op, "act", lambda e, y=y, tab=tab: e.activation(out=tab[:].rearrange("p b j -> p (b j)"), in_=y, func=AF.Sin, scale=6.283185),
                 reads=[t_x_tab], writes=[t_tab])
        def evac_pair(j0, bank0):
            for s in range(2):
                j = j0 + s
                pb = bank0 + s
                for c in range(NCH):
                    dst = hT[:, c, j * 128:(j + 1) * 128]
                    src = PBh[pb][:, c * 128:(c + 1) * 128]
                    if s == 0:
                        S.op("act", lambda e, dst=dst, src=src, c=c: e.activation(out=dst, in_=src, func=AF.Identity, scale=gmod[:, c:c + 1],
                                                                                  bias=modT[:, c:c + 1]),
                             reads=[t_PB[pb], t_mod], writes=[t_hT])
                    else:
                        S.op("dve", lambda e, dst=dst, src=src, c=c: e.tensor_scalar(out=dst, in0=src, scalar1=gmod[:, c:c + 1], scalar2=modT[:, c:c + 1],
                                                                                    op0=ALU.mult, op1=ALU.add),
                             reads=[t_PB[pb], t_mod], writes=[t_hT])

        prev = None
        for pi, j0 in enumerate(range(0, 16, 2)):
            bank0 = 2 * (pi % 2)
            for s in range(2):
                row0 = (SB * 16 + j0 + s) * 128
                S.dma("sp", lambda e, s=s, row0=row0: e.dma_start(out=xn2[:, s, :], in_=x_d[row0:row0 + 128, :]), writes=[t_xn[s]])
            for s in range(2):
                S.op("act", lambda e, s=s: e.activation(out=x_sq, in_=xn2[:, s, :], func=AF.Square, accum_out=ss[:, s:s + 1]),
                     reads=[t_xn[s]], writes=[t_x_sq, t_ss])
            S.op("act", lambda e: e.activation(out=sd[:], in_=ss[:], func=AF.Sqrt, scale=1.0 / 1024.0, bias=EPS), reads=[t_ss], writes=[t_ss])
            S.op("dve", lambda e: e.reciprocal(out=rs2[:], in_=sd[:]), reads=[t_ss], writes=[t_ss])
            for s in range(2):
                pb = bank0 + s
                S.op("dve", lambda e, s=s: e.tensor_scalar(out=x_xb[s], in0=xn2[:, s, :], scalar1=rs2[:, s:s + 1], scalar2=None, op0=ALU.mult),
                     reads=[t_xn[s], t_ss], writes=[t_x_xb[s]])
                for c in range(NCH):
                    S.op("pe", lambda e, s=s, c=c, pb=pb: e.transpose(PBh[pb][:, c * 128:(c + 1) * 128], x_xb[s][:, c * 128:(c + 1) * 128], ident[:]),
                         reads=[t_x_xb[s], t_id], writes=[t_PB[pb]], inc=(c == NCH - 1))
            if prev is not None:
                evac_pair(*prev)
            prev = (j0, bank0)
            tab_emit(6)
        evac_pair(*prev)
        tab_emit(1000)

    wq_state = {"i": 0}

    def attention(SB, half):
        own = SB >= 1
        ctxs = []
        bi = 0
        for g in (2, 1, 0):
            blocks = list(range(16)) if own else HALO_BLOCKS[g]
            for k_, lb in enumerate(blocks):
                d = DIL[g]
                n, r = divmod(lb, d)
                gb = SB * 16 + lb
                tok0 = 128 * n * d + r
                ctxs.append(dict(g=g, d=d, lb=lb, gb=gb, par=bi % 2, first=(k_ == 0),
                                 tsl=slice(tok0, tok0 + 127 * d + 1, d), slot=gb % RSL[g], pslot=(gb - d) % RSL[g]))
                bi += 1
        gparts = {}

        def st_P(cx, which=("q", "k", "v")):
            g, par, tsl = cx["g"], cx["par"], cx["tsl"]
            if cx["first"] and g not in gparts:
                h0 = 8 * g + 4 * half
                parts = {}
                for name, base in (("q", COL_Q), ("k", COL_K), ("v", COL_V)):
                    if name == "q" and not own:
                        continue
                    i = wq_state["i"] % 5
                    wq_state["i"] += 1
                    col0 = base + h0 * 64
                    S.dma("pool", lambda e, i=i, col0=col0: e.dma_start(
                        out=a_wq[i], in_=win_d[:, col0:col0 + 256].rearrange("(c p) n -> p c n", p=128)), writes=[t_a_wq[i]])
                    parts[name] = (a_wq[i], t_a_wq[i])
                gparts[g] = parts
            parts = gparts[g]
            pqk = PB[par]
            pv = PB[2][:, par * 256:(par + 1) * 256]
            for name in which:
                if name not in parts:
                    continue
                wt, twt = parts[name]
                if name == "q":
                    o, to = pqk[:, 0:256], t_pqk[par]
                elif name == "k":
                    o, to = pqk[:, 256:512], t_pqk[par]
                else:
                    o, to = pv, t_pv[par]
                for c in range(NCH):
                    S.op("pe", lambda e, o=o, wt=wt, c=c, tsl=tsl: e.matmul(o, lhsT=hT[:, c, tsl], rhs=wt[:, c, :],
                                                                          start=(c == 0), stop=(c == NCH - 1)),
                         reads=[t_hT, twt], writes=[to], inc=(c == NCH - 1))

        def st_EP(cx):
            g, par, slot, lb = cx["g"], cx["par"], cx["slot"], cx["lb"]
            vr, tvr = VR[g, half], t_VR[g, half]
            pqk = PB[par]
            pv = PB[2][:, par * 256:(par + 1) * 256]
            S.op("act", lambda e, vr=vr, slot=slot, pv=pv: e.activation(out=vr[:, slot, :, :], in_=pv.rearrange("p (h e) -> p h e", e=64), func=AF.Copy),
                 reads=[t_pv[par]], writes=[tvr[slot]])
            hs = slice(0, 8) if own else slice(4, 8)
            nh = 8 if own else 4
            P3 = pqk.rearrange("p (h e) -> p h e", e=64)
            Q3 = a_qk[par].rearrange("p (h e) -> p h e", e=64)
            S.op("act", lambda e, P3=P3, Q3=Q3, hs=hs: e.activation(out=Q3[:, hs, 16:64], in_=P3[:, hs, 16:64], func=AF.Copy),
                 reads=[t_pqk[par]], writes=[t_a_qk[par]])
            tb = g * 16 + lb
            cosb = tcos[:, tb, :].unsqueeze(1).broadcast_to([128, nh, 8])
            sinb = tsin[:, tb, :].unsqueeze(1).broadcast_to([128, nh, 8])
            t1, t2 = P3[:, hs, 0:8], P3[:, hs, 8:16]
            rt = [a_rt[:, k, 0:nh, :] for k in range(4)]
            for k, (ta_, tb_) in enumerate(((t1, cosb), (t2, sinb), (t2, cosb), (t1, sinb))):
                S.op("dve", lambda e, ta_=ta_, tb_=tb_, k=k, rt=rt: e.tensor_tensor(out=rt[k], in0=ta_, in1=tb_, op=ALU.mult),
                     reads=[t_pqk[par], t_tab], writes=[t_a_rt[k]])
            S.op("dve", lambda e, Q3=Q3, hs=hs, rt=rt: e.tensor_tensor(out=Q3[:, hs, 0:8], in0=rt[0], in1=rt[1], op=ALU.subtract),
                 reads=[t_a_rt[0], t_a_rt[1]], writes=[t_a_qkr[par]])
            S.op("dve", lambda e, Q3=Q3, hs=hs, rt=rt: e.tensor_tensor(out=Q3[:, hs, 8:16], in0=rt[2], in1=rt[3], op=ALU.add),
                 reads=[t_a_rt[2], t_a_rt[3]], writes=[t_a_qkr[par]])

        def st_TR(cx):
            g, par, slot = cx["g"], cx["par"], cx["slot"]
            kr, tkr = KR[g, half], t_KR[g, half]
            ptr = PBh[3][:, par * 512:(par + 1) * 512].rearrange("p (j t) -> p j t", j=4)
            js = range(4) if own else range(2, 4)
            for j in js:
                S.op("pe", lambda e, ptr=ptr, j=j, par=par: e.transpose(ptr[:, j, :], a_qk[par][:, j * 128:(j + 1) * 128], ident[:]),
                     reads=[t_a_qk[par], t_a_qkr[par], t_id], writes=[t_ptr[par]], inc=(j == 3))
            S.op("act", lambda e, ptr=ptr, kr=kr, slot=slot: e.activation(out=kr[:, slot, :, :], in_=ptr[:, 2:4, :], func=AF.Copy),
                 reads=[t_ptr[par]], writes=[tkr[slot]])
            if own:
                S.op("dve", lambda e, ptr=ptr, par=par: e.tensor_copy(out=a_QT[par][0:64, :, 0, :], in_=ptr[0:64, 0:2, :]), reads=[t_ptr[par]], writes=[t_a_QT[par]])
                S.op("dve", lambda e, ptr=ptr, par=par: e.tensor_copy(out=a_QT[par][64:128, :, 1, :], in_=ptr[64:128, 0:2, :]), reads=[t_ptr[par]], writes=[t_a_QT[par]])

        def st_QK(cx):
            g, par, slot, pslot, gb, d = cx["g"], cx["par"], cx["slot"], cx["pslot"], cx["gb"], cx["d"]
            kr, tkr = KR[g, half], t_KR[g, half]
            mk = maskb[:, 256:512] if (gb - d) < 16 else maskb[:, 0:256]
            for p in range(2):
                pss = PB[4 + p]
                for hh in range(2):
                    rows = slice(hh * 64, hh * 64 + 64)
                    S.op("pe", lambda e, pss=pss, hh=hh, mk=mk: e.matmul(pss[:, hh * 256:(hh + 1) * 256], lhsT=ident[:], rhs=mk, start=True, stop=False),
                         reads=[t_id, t_mask], writes=[t_pss[p]], inc=False)
                    S.op("pe", lambda e, pss=pss, hh=hh, rows=rows, p=p, pslot=pslot, kr=kr, par=par: e.matmul(
                        pss[:, hh * 256:hh * 256 + 128], lhsT=kr[:, pslot, p, :], rhs=a_QT[par][:, p, hh, :], start=False, stop=False),
                        reads=[tkr[pslot], t_a_QT[par]], writes=[t_pss[p]], inc=False)
                    S.op("pe", lambda e, pss=pss, hh=hh, rows=rows, p=p, slot=slot, kr=kr, par=par: e.matmul(
                        pss[:, hh * 256 + 128:hh * 256 + 256], lhsT=kr[:, slot, p, :], rhs=a_QT[par][:, p, hh, :], start=False, stop=True),
                        reads=[tkr[slot], t_a_QT[par]], writes=[t_pss[p]], inc=(hh == 1))
                S.op("act", lambda e, pss=pss, p=p, par=par: e.activation(
                    out=a_PT[par][:, 2 * p:2 * p + 2, :].rearrange("p h t -> p (h t)"), in_=pss[:, :], func=AF.Exp, scale=0.125),
                    reads=[t_pss[p]], writes=[t_a_PT[par][p]])

        def st_PV(cx):
            g, par, slot, pslot, tsl = cx["g"], cx["par"], cx["slot"], cx["pslot"], cx["tsl"]
            vr, tvr = VR[g, half], t_VR[g, half]
            pso = PB[6 + par][:, :].rearrange("p (k a t) -> p k a t", k=2, a=2)
            for p in range(2):
                for hh in range(2):
                    h4 = 2 * p + hh
                    rows = slice(hh * 64, hh * 64 + 64)
                    tp = (0, 64) if hh else None
                    last = (p == 1 and hh == 1)
                    S.op("pe", lambda e, pso=pso, rows=rows, p=p, h4=h4, tp=tp, vr=vr, pslot=pslot, par=par: e.matmul(
                        pso[rows, 0, p, :], lhsT=vr[:, pslot, h4, :], rhs=a_PT[par][:, h4, 0:128], start=True, stop=False, tile_position=tp),
                        reads=[tvr[pslot], t_a_PT[par][p]], writes=[t_pso[par]], inc=False)
                    S.op("pe", lambda e, pso=pso, rows=rows, p=p, h4=h4, tp=tp, vr=vr, slot=slot, par=par: e.matmul(
                        pso[rows, 0, p, :], lhsT=vr[:, slot, h4, :], rhs=a_PT[par][:, h4, 128:256], start=False, stop=True, tile_position=tp),
                        reads=[tvr[slot], t_a_PT[par][p]], writes=[t_pso[par]], inc=False)
                    S.op("pe", lambda e, pso=pso, rows=rows, p=p, h4=h4, tp=tp, par=par: e.matmul(
                        pso[rows, 1, p, :], lhsT=ones64[:], rhs=a_PT[par][:, h4, 0:128], start=True, stop=False, tile_position=tp),
                        reads=[t_ones, t_a_PT[par][p]], writes=[t_pso[par]], inc=False)
                    S.op("pe", lambda e, pso=pso, rows=rows, p=p, h4=h4, tp=tp, par=par: e.matmul(
                        pso[rows, 1, p, :], lhsT=ones64[:], rhs=a_PT[par][:, h4, 128:256], start=False, stop=True, tile_position=tp),
                        reads=[t_ones, t_a_PT[par][p]], writes=[t_pso[par]], inc=last)
        def st_ACC(cx):
            g, par, tsl = cx["g"], cx["par"], cx["tsl"]
            pso = PB[6 + par][:, :].rearrange("p (k a t) -> p k a t", k=2, a=2)
            an, ad = a_accn[:, :, tsl], a_accd[:, :, tsl]
            if g == 2:
                S.op("dve", lambda e, an=an, pso=pso: e.tensor_copy(out=an, in_=pso[:, 0, :, :]), reads=[t_pso[par]], writes=[t_a_acc])
                S.op("dve", lambda e, ad=ad, pso=pso: e.tensor_copy(out=ad, in_=pso[:, 1, :, :]), reads=[t_pso[par]], writes=[t_a_acc])
            else:
                S.op("dve", lambda e, an=an, pso=pso: e.tensor_tensor(out=an, in0=pso[:, 0, :, :], in1=an, op=ALU.add),
                     reads=[t_pso[par], t_a_acc], writes=[t_a_acc])
                S.op("dve", lambda e, ad=ad, pso=pso: e.tensor_tensor(out=ad, in0=pso[:, 1, :, :], in1=ad, op=ALU.add),
                     reads=[t_pso[par], t_a_acc], writes=[t_a_acc])

        nb = len(ctxs)
        if own:
            for i in range(2):
                S.op("dve", lambda e, i=i: e.memset(a_QT[i], 0.0), writes=[t_a_QT[i]])
        st_P(ctxs[0]); st_EP(ctxs[0])
        for b in range(nb):
            if b + 1 < nb:
                st_P(ctxs[b + 1], which=("q", "k"))
            st_TR(ctxs[b])
            if b + 1 < nb:
                st_P(ctxs[b + 1], which=("v",))
            if own and b >= 1:
                st_PV(ctxs[b - 1])
            if b + 1 < nb:
                st_EP(ctxs[b + 1])
            if own and b >= 1:
                st_ACC(ctxs[b - 1])
            if own:
                st_QK(ctxs[b])
        if own:
            st_PV(ctxs[nb - 1])
            st_ACC(ctxs[nb - 1])
        if not own:
            return
        FIN_ALIAS = t_a_qk + t_a_qkr + t_a_QT + t_a_PT[0] + t_a_PT[1]
        S.op("dve", lambda e: e.memset(scratch[:], 0.0), writes=FIN_ALIAS + [t_a_fin, t_scr])
        for p in range(2):
            gp = 2 * half + p
            wz, twz = load_chunk(win_d[:, COL_ZA + gp * 128:COL_ZA + (gp + 1) * 128])
            for tt in range(4):
                pz = PB[tt % 2]
                tsl = slice(tt * 512, (tt + 1) * 512)
                for c in range(NCH):
                    S.op("pe", lambda e, pz=pz, wz=wz, c=c, tsl=tsl: e.matmul(pz[:, :], lhsT=wz[:, c, :], rhs=hT[:, c, tsl], start=(c == 0), stop=(c == NCH - 1)),
                         reads=[twz, t_hT], writes=[t_pqk[tt % 2]], inc=(c == NCH - 1))
                S.op("act", lambda e, pz=pz: e.activation(out=pz[:, :], in_=pz[:, :], func=AF.Silu), reads=[t_pqk[tt % 2]], writes=[t_pqk[tt % 2]])
                prd, tprd = PB[2 + tt % 2], t_PB[2 + tt % 2]
                S.op("dve", lambda e, p=p, tsl=tsl, prd=prd: e.reciprocal(out=prd[:, :], in_=a_accd[:, p, tsl]), reads=[t_a_acc], writes=[tprd])
                S.op("dve", lambda e, p=p, tsl=tsl, prd=prd: e.tensor_tensor(out=a_ft, in0=a_accn[:, p, tsl], in1=prd[:, :], op=ALU.mult),
                     reads=[t_a_acc, tprd], writes=[t_a_fin])
                S.op("dve", lambda e, gp=gp, tsl=tsl, pz=pz: e.tensor_tensor(out=agT[:, gp, tsl], in0=a_ft, in1=pz[:, :], op=ALU.mult),
                     reads=[t_a_fin, t_pqk[tt % 2]], writes=[t_agT[gp]])
        S.op("dve", lambda e: e.memset(scratch[:], 0.0), writes=FIN_ALIAS + [t_a_fin, t_scr])

    def glu_loads(c):
        return (load_chunk(win_d[:, COL_GA + c * 128:COL_GA + (c + 1) * 128]),
                load_chunk(win_d[:, COL_GB + c * 128:COL_GB + (c + 1) * 128]))

    def glu_chunk(c, tsl, ncol, uo, t_uo, flag, wts=None):
        (wa, twa), (wb_, twb_) = wts if wts is not None else glu_loads(c)
        pa, pb = PB[2 * (c % 2)], PB[2 * (c % 2) + 1]
        ta, tb = t_PB[2 * (c % 2)], t_PB[2 * (c % 2) + 1]
        for k in range(NCH):
            S.op("pe", lambda e, pa=pa, wa=wa, k=k: e.matmul(pa[:, 0:ncol], lhsT=wa[:, k, :], rhs=hT[:, k, tsl], start=(k == 0), stop=(k == NCH - 1)),
                 reads=[twa, t_hT], writes=[ta], inc=(k == NCH - 1))
        for k in range(NCH):
            S.op("pe", lambda e, pb=pb, wb_=wb_, k=k: e.matmul(pb[:, 0:ncol], lhsT=wb_[:, k, :], rhs=hT[:, k, tsl], start=(k == 0), stop=(k == NCH - 1)),
                 reads=[twb_, t_hT], writes=[tb], inc=(k == NCH - 1))
        th = c_th[c % 2]
        S.op("act", lambda e, th=th, pb=pb: e.activation(out=th[:, 0:ncol], in_=pb[:, 0:ncol], func=AF.Tanh, scale=0.5),
             reads=[tb], writes=[t_c_th[c % 2]])
        S.op("dve", lambda e, th=th, pa=pa, uo=uo: e.scalar_tensor_tensor(out=uo, in0=th[:, 0:ncol], scalar=1.0, in1=pa[:, 0:ncol],
                                                                          op0=ALU.add, op1=ALU.mult),
             reads=[t_c_th[c % 2], ta], writes=[t_uo])
        if flag:
            S.op("dve", lambda e, uo=uo: e.tensor_scalar(out=uo, in0=uo, scalar1=hflag[:, 0:1],
                                                         scalar2=None, op0=ALU.mult), reads=[t_uo, t_hflag], writes=[t_uo])

    def halo_u():
        switch_phase(CONV_T + t_PB)
        for c in range(NCH):
            glu_chunk(c, slice(SBT - 32, SBT), 32, uhist[:, c, :], t_uhist[c], True)

    def conv_tile(SB, tt):
        tsl = slice(tt * 512, (tt + 1) * 512)
        S.op("dve", lambda e: e.tensor_copy(out=c_uT[:, :, 0:32], in_=uhist[:, :, :]), reads=t_uhist, writes=t_c_uT + t_c_mgT)
        cw3 = cwh[:].rearrange("p (j c) -> p j c", c=8)

        gw = {}

        def diag_load(c):
            dg, tdg = c_diag[c % 2], t_c_diag[c % 2]
            S.dma("sp", lambda e, dg=dg, c=c: e.dma_start(out=dg, in_=dscr[c].rearrange("p (j n) -> p j n", j=31)), reads=[t_dscr], writes=[tdg])

        def st_G(c):
            if c + 1 < NCH:
                gw[c + 1] = glu_loads(c + 1)
            if not (c < 2 and tt > 0):
                diag_load(c)
            glu_chunk(c, tsl, 512, c_uT[:, c, 32:544], t_c_uT[c], False, wts=gw.pop(c))
            S.op("dve", lambda e, c=c: e.tensor_copy(out=uhist[:, c, :], in_=c_uT[:, c, 512:544]), reads=[t_c_uT[c]], writes=[t_uhist[c]])

        def st_C(c):
            pc, tpc = PB[4 + (c % 2)], t_PB[4 + (c % 2)]
            dg, tdg = c_diag[c % 2], t_c_diag[c % 2]
            for j in range(NTAP_PE):
                S.op("pe", lambda e, pc=pc, dg=dg, c=c, j=j: e.matmul(pc[:, :], lhsT=dg[:, j, :], rhs=c_uT[:, c, 2 + j:514 + j], start=(j == 0), stop=(j == NTAP_PE - 1)),
                     reads=[tdg, t_c_uT[c]], writes=[tpc], inc=(j == NTAP_PE - 1))
            for j in range(NTAP_PE, 31):
                S.op("dve", lambda e, pc=pc, c=c, j=j: e.scalar_tensor_tensor(out=pc[:, :], in0=c_uT[:, c, 2 + j:514 + j], scalar=cwh[:, j * 8 + c:j * 8 + c + 1],
                                                                              in1=pc[:, :], op0=ALU.mult, op1=ALU.add),
                     reads=[t_c_uT[c], t_cwh, tpc], writes=[tpc])
            cb = vT[:, 8 + c:9 + c]
            S.op("act", lambda e, pc=pc, c=c, cb=cb: e.activation(out=c_cv[:, c, :], in_=pc[:, :], func=AF.Identity, bias=cb), reads=[tpc, t_vT], writes=[t_c_cv[c], t_c_ucT[c]])
            S.op("act", lambda e, pc=pc, c=c, cb=cb: e.activation(out=c_cvb[c % 2], in_=pc[:, :], func=AF.Identity, bias=cb), reads=[tpc, t_vT], writes=[t_c_cvb[c % 2]])
            S.op("act", lambda e, pc=pc, c=c, cb=cb: e.activation(out=c_sqb[c % 2], in_=pc[:, :], func=AF.Square, bias=cb), reads=[tpc, t_vT], writes=[t_c_sqb[c % 2]])

        def st_S(c):
            S.op("pe", lambda e, c=c: e.matmul(PB[6][:, :], lhsT=onesdiv[:], rhs=c_cvb[c % 2], start=(c == 0), stop=(c == NCH - 1)),
                 reads=[t_ones, t_c_cvb[c % 2]], writes=[t_PB[6]], inc=True)
            S.op("pe", lambda e, c=c: e.matmul(PB[7][:, :], lhsT=onesdiv[:], rhs=c_sqb[c % 2], start=(c == 0), stop=(c == NCH - 1)),
                 reads=[t_ones, t_c_sqb[c % 2]], writes=[t_PB[7]], inc=True)

        gw[0] = glu_loads(0)
        st_G(0)
        for c in range(NCH):
            if c + 1 < NCH:
                st_G(c + 1)
            st_C(c)
            if c >= 1:
                st_S(c - 1)
        st_S(NCH - 1)
        if tt + 1 < 4:
            diag_load(0); diag_load(1)
        S.op("act", lambda e: e.activation(out=c_lt, in_=PB[6][:, :], func=AF.Square), reads=[t_PB[6]], writes=[t_c_lt] + t_c_ot)
        S.op("dve", lambda e: e.tensor_tensor(out=c_var, in0=PB[7][:, :], in1=c_lt, op=ALU.subtract), reads=[t_PB[7], t_c_lt], writes=[t_c_ln] + t_c_ot)
        S.op("act", lambda e: e.activation(out=c_var, in_=c_var, func=AF.Sqrt, bias=EPS), reads=[t_c_ln], writes=[t_c_ln])
        S.op("dve", lambda e: e.reciprocal(out=c_rstd, in_=c_var), reads=[t_c_ln], writes=[t_c_ln])
        S.op("dve", lambda e: e.scalar_tensor_tensor(out=PB[6][:, :], in0=PB[6][:, :], scalar=-1.0, in1=c_rstd, op0=ALU.mult, op1=ALU.mult),
             reads=[t_c_ln], writes=[t_PB[6]])
        S.op("dve", lambda e: e.tensor_copy(out=PB[7][:, :], in_=c_rstd), reads=[t_c_ln], writes=[t_PB[7]])
        szb = [c_szc, c_sl]; t_szb = [t_c_szc, t_c_sl]

        def st_Z(c):
            wz, twz = load_chunk(win_d[:, COL_ZC + c * 128:COL_ZC + (c + 1) * 128])
            pz, tpz = PB[c % 4], t_PB[c % 4]
            for k in range(NCH):
                S.op("pe", lambda e, pz=pz, wz=wz, k=k: e.matmul(pz[:, :], lhsT=wz[:, k, :], rhs=hT[:, k, tsl], start=(k == 0), stop=(k == NCH - 1)),
                     reads=[twz, t_hT], writes=[tpz], inc=(k == NCH - 1))
            S.op("act", lambda e, pz=pz, c=c: e.activation(out=szb[c % 2], in_=pz[:, :], func=AF.Silu), reads=[tpz], writes=[t_szb[c % 2]])

        st_Z(0)
        for c in range(NCH):
            if c + 1 < NCH:
                st_Z(c + 1)
            S.op("dve", lambda e, c=c: e.tensor_tensor(out=c_lt, in0=c_cv[:, c, :], in1=PB[7][:, :], op=ALU.mult), reads=[t_c_cv[c], t_PB[7]], writes=[t_c_lt])
            S.op("dve", lambda e: e.tensor_tensor(out=c_lt, in0=c_lt, in1=PB[6][:, :], op=ALU.add), reads=[t_c_lt, t_PB[6]], writes=[t_c_lt])
            psl, tpsl = PB[4 + (c % 2)], t_PB[4 + (c % 2)]
            S.op("act", lambda e, c=c, psl=psl: e.activation(out=psl[:, :], in_=c_lt, func=AF.Silu, scale=vT[:, 16 + c:17 + c], bias=vT[:, 24 + c:25 + c]),
                 reads=[t_c_lt, t_vT], writes=[tpsl])
            S.op("dve", lambda e, c=c, psl=psl: e.tensor_tensor(out=c_ucT[c], in0=szb[c % 2], in1=psl[:, :], op=ALU.mult),
                 reads=[t_szb[c % 2], tpsl], writes=[t_c_ucT[c], t_c_cv[c]])
        for n in range(NCH):
            wgc, twgc = load_chunk(win_d[:, COL_GC + n * 128:COL_GC + (n + 1) * 128])
            wga, twga = load_chunk(win_d[:, COL_GT + n * 128:COL_GT + (n + 1) * 128])
            wco, twco = load_chunk(wco_d[:, n * 128:(n + 1) * 128])
            wao, twao = load_chunk(wao_d[:, n * 128:(n + 1) * 128], kch=4)
            o = 4 * (n % 2)
            pgc, pga, pyc, pya = PB[o], PB[o + 1], PB[o + 2], PB[o + 3]
            tgc, tga, tyc, tya = t_PB[o], t_PB[o + 1], t_PB[o + 2], t_PB[o + 3]
            for k in range(NCH):
                S.op("pe", lambda e, pgc=pgc, wgc=wgc, k=k: e.matmul(pgc[:, :], lhsT=wgc[:, k, :], rhs=hT[:, k, tsl], start=(k == 0), stop=(k == NCH - 1)),
                     reads=[twgc, t_hT], writes=[tgc], inc=(k == NCH - 1))
            for k in range(NCH):
                S.op("pe", lambda e, pga=pga, wga=wga, k=k: e.matmul(pga[:, :], lhsT=wga[:, k, :], rhs=hT[:, k, tsl], start=(k == 0), stop=(k == NCH - 1)),
                     reads=[twga, t_hT], writes=[tga], inc=(k == NCH - 1))
            for k in range(NCH):
                S.op("pe", lambda e, pyc=pyc, wco=wco, k=k: e.matmul(pyc[:, :], lhsT=wco[:, k, :], rhs=c_ucT[k], start=(k == 0), stop=(k == NCH - 1)),
                     reads=[twco, t_c_ucT[k]], writes=[tyc], inc=(k == NCH - 1))
            for k in range(4):
                S.op("pe", lambda e, pya=pya, wao=wao, k=k: e.matmul(pya[:, :], lhsT=wao[:, k, :], rhs=agT[:, k, tsl], start=(k == 0), stop=(k == 3)),
                     reads=[twao, t_agT[k]], writes=[tya], inc=(k == 3))
            thc, tha = c_th[0], c_th[1]
            S.op("act", lambda e, pgc=pgc: e.activation(out=thc, in_=pgc[:, :], func=AF.Tanh, scale=0.5), reads=[tgc], writes=[t_c_th[0]])
            S.op("act", lambda e, pga=pga: e.activation(out=tha, in_=pga[:, :], func=AF.Tanh, scale=0.5), reads=[tga], writes=[t_c_th[1]])
            S.op("dve", lambda e, pyc=pyc: e.scalar_tensor_tensor(out=pyc[:, :], in0=thc, scalar=1.0, in1=pyc[:, :], op0=ALU.add, op1=ALU.mult),
                 reads=[t_c_th[0], tyc], writes=[tyc])
            S.op("dve", lambda e, pya=pya: e.scalar_tensor_tensor(out=c_m2, in0=tha, scalar=1.0, in1=pya[:, :], op0=ALU.add, op1=ALU.mult),
                 reads=[t_c_th[1], tya], writes=[t_c_m2])
            S.op("dve", lambda e, pyc=pyc, n=n: e.tensor_tensor(out=c_mgT[n], in0=c_m2, in1=pyc[:, :], op=ALU.add),
                 reads=[t_c_m2, tyc], writes=[t_c_mgT[n]] + t_c_uT)
        def x_loads(SB_, tt_, s0_):
            key = (SB_, tt_, s0_)
            if key in xpref:
                return
            xpref.add(key)
            for s in range(2):
                row_in = SB_ * SBT + tt_ * 512 + (s0_ + s) * 128
                S.dma("sp", lambda e, s=s, row_in=row_in: e.dma_start(out=xn2[:, s, :], in_=x_d[row_in:row_in + 128, :]), writes=[t_xn[s]])

        for s0 in range(0, 4, 2):
            x_loads(SB, tt, s0)
            for s in range(2):
                sub = s0 + s
                o = 4 * s
                for hh in range(2):
                    py, tpy = PB[o + hh], t_PB[o + hh]
                    for k in range(NCH):
                        S.op("pe", lambda e, py=py, k=k, sub=sub, hh=hh: e.matmul(py[:, :], lhsT=c_mgT[k][:, sub * 128:(sub + 1) * 128],
                                                                                rhs=wog[:, k, hh * 512:(hh + 1) * 512], start=(k == 0), stop=(k == NCH - 1)),
                             reads=[t_c_mgT[k], t_wog], writes=[tpy], inc=(k == NCH - 1))
                    S.op("dve", lambda e, py=py, s=s, hh=hh: e.tensor_tensor(out=c_ot[:, s, hh * 512:(hh + 1) * 512], in0=py[:, :],
                                                                             in1=xn2[:, s, hh * 512:(hh + 1) * 512], op=ALU.add),
                         reads=[tpy, t_xn[s]], writes=[t_c_ot[s], t_c_ln, t_c_lt])
                S.op("act", lambda e, s=s: e.activation(out=c_sl.bitcast(BF16)[:, 0:1024], in_=c_ot[:, s, :], func=AF.Square, accum_out=fsc[:, s:s + 1]),
                     reads=[t_c_ot[s]], writes=[t_c_sl, t_fsc])
            if s0 == 0:
                x_loads(SB, tt, 2)
            elif tt + 1 < 4:
                x_loads(SB, tt + 1, 0)
            S.op("act", lambda e: e.activation(out=fsc[:, 2:4], in_=fsc[:, 0:2], func=AF.Sqrt, scale=1.0 / 1024.0, bias=EPS), reads=[t_fsc], writes=[t_fsc])
            S.op("dve", lambda e: e.reciprocal(out=fsc[:, 4:6], in_=fsc[:, 2:4]), reads=[t_fsc], writes=[t_fsc])
            for s in range(2):
                sub = s0 + s
                row_out = (SB - 1) * SBT + tt * 512 + sub * 128
                S.op("dve", lambda e, s=s: e.scalar_tensor_tensor(out=c_ot[:, s, :], in0=c_ot[:, s, :], scalar=fsc[:, 4 + s:5 + s], in1=fgb[:],
                                                                  op0=ALU.mult, op1=ALU.mult), reads=[t_fsc, t_fgb], writes=[t_c_ot[s]])
                S.dma("sp", lambda e, s=s, row_out=row_out: e.dma_start(out=out_d[row_out:row_out + 128, :], in_=c_ot[:, s, :]), reads=[t_c_ot[s]])

    xpref = set()
    import os
    kstop = int(os.environ.get("KSTOP", "999"))
    step = [0]

    def stop():
        step[0] += 1
        print("STEP", step[0], "ops so far", S.total)
        return step[0] > kstop

    def main():
        if stop(): return
        for SB in range(NSB):
            phase_x(SB)
            if stop(): return
            switch_phase(ATT_T + ATT_P + t_PB)
            attention(SB, 0)
            if stop(): return
            attention(SB, 1)
            if stop(): return
            if SB == 0:
                halo_u()
                if stop(): return
            else:
                switch_phase(CONV_T + t_PB)
                for tt in range(4):
                    conv_tile(SB, tt)
                    if stop(): return
    main()
    S.wait_all("sp", t_xn + t_c_ot + state["phase_T"] + [t_hT, t_wog, t_fgb, t_mask, t_hflag, t_cwh, t_mod, t_tab])
    S.emit()
    return nc


def make_in_maps(x, c, positions, norm_g, w_ada, b_ada, w_in, conv_w, conv_b, conv_ln_g, conv_ln_b,
                 w_conv_out, w_att_out, w_o, final_g):
    f32 = np.float32
    ki = np.arange(128)[:, None]
    qi = np.arange(128)[None, :]
    prev = np.where(ki >= qi, 0.0, NEG).astype(f32)
    cur = np.where(ki <= qi, 0.0, NEG).astype(f32)
    allneg = np.full((128, 128), NEG, f32)
    shared = dict(
        cw=np.ascontiguousarray(np.asarray(conv_w[0], f32).reshape(248, 128)),
        b_ada=np.ascontiguousarray(np.asarray(b_ada[0], f32).reshape(1, 3 * D)),
        w_ada=np.ascontiguousarray(np.asarray(w_ada[0], f32)),
        w_in=np.ascontiguousarray(np.asarray(w_in[0], f32)),
        w_conv_out=np.ascontiguousarray(np.asarray(w_conv_out[0], f32)),
        w_att_out=np.ascontiguousarray(np.asarray(w_att_out[0], f32)),
        w_o=np.ascontiguousarray(np.asarray(w_o[0], f32)),
        final_g=np.ascontiguousarray(np.asarray(final_g, f32).reshape(1, D)),
    )
    maps = []
    for core in range(8):
        b, hf = divmod(core, 2)
        xl = np.zeros((TOKL, D), f32)
        pl = np.zeros((TOKL,), np.int32)
        if hf == 0:
            xl[SBT:] = x[b, 0:OWN]
            pl[SBT:] = positions[b, 0:OWN]
        else:
            xl[:] = x[b, OWN - SBT:]
            pl[:] = positions[b, OWN - SBT:]
        vecs = np.concatenate([np.asarray(norm_g[0], f32).reshape(8, 128), np.asarray(conv_b[0], f32).reshape(8, 128),
                               np.asarray(conv_ln_g[0], f32).reshape(8, 128), np.asarray(conv_ln_b[0], f32).reshape(8, 128),
                               np.asarray(c[b], f32).reshape(8, 128)], 0)
        masks = np.concatenate([prev, cur, allneg if hf == 0 else prev, cur], 1)
        m = dict(shared)
        m.update(x=xl, pos=pl, vecs=np.ascontiguousarray(vecs), masks=np.ascontiguousarray(masks),
                 hflag=np.full((128, 1), float(hf), f32))
        maps.append(m)
    return maps


_NC_CACHE = {}


def kernel(x, c, positions, norm_g, w_ada, b_ada, w_in, conv_w, conv_b, conv_ln_g, conv_ln_b,
           w_conv_out, w_att_out, w_o, final_g):
    x = np.asarray(x); positions = np.asarray(positions)
    maps = make_in_maps(x, np.asarray(c), positions, np.asarray(norm_g), np.asarray(w_ada), np.asarray(b_ada), np.asarray(w_in),
                        np.asarray(conv_w), np.asarray(conv_b), np.asarray(conv_ln_g), np.asarray(conv_ln_b),
                        np.asarray(w_conv_out), np.asarray(w_att_out), np.asarray(w_o), np.asarray(final_g))
    nc = build_program()
    res = run_bass_kernel_spmd(nc, maps, core_ids=list(range(8)))
    out = np.empty((4, 8192, D), np.float32)
    for core in range(8):
        b, hf = divmod(core, 2)
        out[b, hf * OWN:(hf + 1) * OWN] = res.results[core]["out"]
    return out
```
